# Optimizing a Trainium2 kernel written in Bass

```python
import math
import jax, jax.numpy as jnp
from jax import lax
import numpy as np

D_MODEL = 1024
BATCH = 32
SEQ = 2048
DEPTH = 2

GRID_W = 64
CTX_LEN = 256
HEAD_DIM = 64
A_HEADS = D_MODEL // (2 * HEAD_DIM)
A_WIDTH = A_HEADS * 2 * HEAD_DIM
A_IN_WIDTH = 4 * A_WIDTH
B_HEADS = D_MODEL // HEAD_DIM
B_KV_HEADS = 4
B_GROUP = B_HEADS // B_KV_HEADS
B_WIDTH = B_HEADS * HEAD_DIM
B_KV_WIDTH = B_KV_HEADS * HEAD_DIM
B_IN_WIDTH = 2 * B_WIDTH + 2 * B_KV_WIDTH
MIX_WIDTH = A_WIDTH
WINDOW = 128
Q_BLOCK = 128
BAND = Q_BLOCK + 2 * WINDOW
ROPE_THETA = 10000.0
NORM_EPS = 1e-6
SUBLN_EPS = 1e-5
NEG_INF = -1e30
ATTN_SCALE = HEAD_DIM ** -0.5
N_A_LAYERS = (DEPTH + 1) // 2
N_B_LAYERS = DEPTH // 2

kernel_name = 'hybrid_diffattn_windowgqa_ctxprefix_dit'


def rms_norm(x, g, eps=NORM_EPS):
    x32 = x.astype(jnp.float32)
    y = x32 * lax.rsqrt(jnp.mean(x32 * x32, axis=-1, keepdims=True) + eps)
    return (y * g.astype(jnp.float32)).astype(x.dtype)


def modulate(h, shift, scale):
    return h * (1.0 + scale) + shift


def axial_angles(rows, rot_dim=HEAD_DIM):
    row = jnp.repeat(jnp.arange(rows, dtype=jnp.int32), GRID_W).astype(jnp.float32)
    col = jnp.tile(jnp.arange(GRID_W, dtype=jnp.int32), rows).astype(jnp.float32)
    axis_dim = rot_dim // 2
    inv_freq = ROPE_THETA ** (-jnp.arange(0, axis_dim, 2, dtype=jnp.float32) / axis_dim)
    return row[:, None] * inv_freq, col[:, None] * inv_freq


def rope_1d(x, ang):
    ang = ang.reshape(ang.shape[:1] + (1,) * (x.ndim - 3) + ang.shape[1:])
    cos, sin = jnp.cos(ang), jnp.sin(ang)
    x1, x2 = jnp.split(x.astype(jnp.float32), 2, axis=-1)
    return jnp.concatenate([x1 * cos - x2 * sin, x2 * cos + x1 * sin], axis=-1).astype(x.dtype)


def rope_axial(x, ang_r, ang_c):
    xr, xc = jnp.split(x, 2, axis=-1)
    return jnp.concatenate([rope_1d(xr, ang_r), rope_1d(xc, ang_c)], axis=-1)


def diff_attention_mixer(hx, hc, w_in, lq1, lk1, lq2, lk2, subln_g, layer_idx, with_ctx_out, ang_r, ang_c):
    B, S, _ = hx.shape
    nblk = S // Q_BLOCK
    lam_init = 0.8 - 0.6 * math.exp(-0.3 * layer_idx)
    f32 = jnp.float32
    lam = (jnp.exp(jnp.sum(lq1.astype(f32) * lk1.astype(f32)))
           - jnp.exp(jnp.sum(lq2.astype(f32) * lk2.astype(f32))) + lam_init)

    def split(p):
        q, k, v, g = jnp.split(p, [A_WIDTH, 2 * A_WIDTH, 3 * A_WIDTH], axis=-1)
        lead = p.shape[:2]
        return (q.reshape(lead + (A_HEADS, 2, HEAD_DIM)),
                k.reshape(lead + (A_HEADS, 2, HEAD_DIM)),
                v.reshape(lead + (A_HEADS, 2 * HEAD_DIM)), g)

    qx, kx, vx, gx = split(hx @ w_in)
    qc, kc, vc, gc = split(hc @ w_in)
    qx = rope_axial(qx, ang_r, ang_c)
    kx = rope_axial(kx, ang_r, ang_c)
    k_all = jnp.concatenate([kc, kx], axis=1)
    v_all = jnp.concatenate([vc, vx], axis=1)

    def attend(qb, keys, vals):
        s = jnp.einsum('bqhmd,bkhmd->bhmqk', qb, keys).astype(f32) * ATTN_SCALE
        p = jax.nn.softmax(s, axis=-1)
        a = p[:, :, 0] - lam * p[:, :, 1]
        return jnp.einsum('bhqk,bkhe->bqhe', a.astype(vals.dtype), vals)

    q_blocks = qx.reshape(B, nblk, Q_BLOCK, A_HEADS, 2, HEAD_DIM).swapaxes(0, 1)
    ox = lax.map(lambda qb: attend(qb, k_all, v_all), q_blocks)
    ox = ox.swapaxes(0, 1).reshape(B, S, A_HEADS, 2 * HEAD_DIM)

    def finish(o, g):
        o = rms_norm(o, subln_g, SUBLN_EPS) * (1.0 - lam_init)
        return o.reshape(o.shape[:2] + (A_WIDTH,)) * jax.nn.silu(g)

    out_x = finish(ox, gx)
    out_c = finish(attend(qc, kc, vc), gc) if with_ctx_out else None
    return out_x, out_c


def window_gqa_mixer(hx, hc, w_in, sink, with_ctx_out, ang_r, ang_c):
    B, S, _ = hx.shape
    nblk = S // Q_BLOCK
    f32 = jnp.float32

    def split(p):
        q, k, v, g = jnp.split(p, [B_WIDTH, B_WIDTH + B_KV_WIDTH, B_WIDTH + 2 * B_KV_WIDTH], axis=-1)
        lead = p.shape[:2]
        return (q.reshape(lead + (B_KV_HEADS, B_GROUP, HEAD_DIM)),
                k.reshape(lead + (B_KV_HEADS, HEAD_DIM)),
                v.reshape(lead + (B_KV_HEADS, HEAD_DIM)), g)

    qx, kx, vx, gx = split(hx @ w_in)
    qc, kc, vc, gc = split(hc @ w_in)
    qx = rope_axial(qx, ang_r, ang_c)
    kx = rope_axial(kx, ang_r, ang_c)
    n_ctx = kc.shape[1]
    sink_f = sink.astype(f32).reshape(B_KV_HEADS, B_GROUP)[None, :, :, None, None]

    def sink_softmax(s):
        sb = jnp.broadcast_to(sink_f, s.shape[:-1] + (1,))
        return jax.nn.softmax(jnp.concatenate([s, sb], axis=-1), axis=-1)[..., :-1]

    pad = ((0, 0), (WINDOW, WINDOW), (0, 0), (0, 0))
    kp = jnp.pad(kx, pad)
    vp = jnp.pad(vx, pad)
    q_blocks = qx.reshape(B, nblk, Q_BLOCK, B_KV_HEADS, B_GROUP, HEAD_DIM).swapaxes(0, 1)

    def block(args):
        qb, i = args
        start = i * Q_BLOCK
        kb = lax.dynamic_slice_in_dim(kp, start, BAND, axis=1)
        vb = lax.dynamic_slice_in_dim(vp, start, BAND, axis=1)
        qpos = start + jnp.arange(Q_BLOCK)
        kpos = start - WINDOW + jnp.arange(BAND)
        mask = ((jnp.abs(qpos[:, None] - kpos[None, :]) <= WINDOW)
                & (kpos >= 0)[None, :] & (kpos < S)[None, :])
        s_band = jnp.einsum('bqhgd,bjhd->bhgqj', qb, kb).astype(f32) * ATTN_SCALE
        s_band = jnp.where(mask, s_band, NEG_INF)
        s_ctx = jnp.einsum('bqhgd,bjhd->bhgqj', qb, kc).astype(f32) * ATTN_SCALE
        p = sink_softmax(jnp.concatenate([s_ctx, s_band], axis=-1)).astype(vb.dtype)
        return (jnp.einsum('bhgqj,bjhd->bqhgd', p[..., :n_ctx], vc)
                + jnp.einsum('bhgqj,bjhd->bqhgd', p[..., n_ctx:], vb))

    ox = lax.map(block, (q_blocks, jnp.arange(nblk)))
    out_x = ox.swapaxes(0, 1).reshape(B, S, B_WIDTH) * jax.nn.silu(gx)
    out_c = None
    if with_ctx_out:
        s = jnp.einsum('bqhgd,bjhd->bhgqj', qc, kc).astype(f32) * ATTN_SCALE
        p = sink_softmax(s).astype(vc.dtype)
        oc = jnp.einsum('bhgqj,bjhd->bqhgd', p, vc)
        out_c = oc.reshape(oc.shape[:2] + (B_WIDTH,)) * jax.nn.silu(gc)
    return out_x, out_c


def setup_inputs(seed: int = 0) -> dict:
    key = jax.random.key(seed)
    ks = jax.random.split(key, 17)
    nrm = jax.random.normal
    f32 = jnp.float32
    return {
        'x': nrm(ks[0], (BATCH, SEQ, D_MODEL), f32),
        'c': nrm(ks[1], (BATCH, D_MODEL), f32),
        'ctx': nrm(ks[2], (BATCH, CTX_LEN, D_MODEL), f32),
        'c_ctx': nrm(ks[3], (D_MODEL,), f32),
        'w_mod': nrm(ks[4], (DEPTH, D_MODEL, 3 * D_MODEL), f32) * D_MODEL ** -0.5,
        'b_mod': 0.01 * nrm(ks[5], (DEPTH, 3 * D_MODEL), f32),
        'norm_g': 1.0 + 0.02 * nrm(ks[6], (DEPTH, D_MODEL), f32),
        'w_o': nrm(ks[7], (DEPTH, MIX_WIDTH, D_MODEL), f32) * MIX_WIDTH ** -0.5,
        'a_w_in': nrm(ks[8], (N_A_LAYERS, D_MODEL, A_IN_WIDTH), f32) * D_MODEL ** -0.5,
        'a_lambda_q1': 0.1 * nrm(ks[9], (N_A_LAYERS, HEAD_DIM), f32),
        'a_lambda_k1': 0.1 * nrm(ks[10], (N_A_LAYERS, HEAD_DIM), f32),
        'a_lambda_q2': 0.1 * nrm(ks[11], (N_A_LAYERS, HEAD_DIM), f32),
        'a_lambda_k2': 0.1 * nrm(ks[12], (N_A_LAYERS, HEAD_DIM), f32),
        'a_subln_g': 1.0 + 0.02 * nrm(ks[13], (N_A_LAYERS, 2 * HEAD_DIM), f32),
        'b_w_in': nrm(ks[14], (N_B_LAYERS, D_MODEL, B_IN_WIDTH), f32) * D_MODEL ** -0.5,
        'b_sink': 0.5 * nrm(ks[15], (N_B_LAYERS, B_HEADS), f32),
        'final_g': 1.0 + 0.02 * nrm(ks[16], (D_MODEL,), f32),
    }


def reference(x, c, ctx, c_ctx, w_mod, b_mod, norm_g, w_o, a_w_in, a_lambda_q1, a_lambda_k1,
              a_lambda_q2, a_lambda_k2, a_subln_g, b_w_in, b_sink, final_g):
    n_tokens = x.shape[1]
    rows = n_tokens // GRID_W
    ang_r, ang_c = axial_angles(rows)
    sc = jax.nn.silu(c)
    sctx = jax.nn.silu(c_ctx)
    for i in range(DEPTH):
        last = i == DEPTH - 1
        shift_x, scale_x, gate_x = jnp.split(sc @ w_mod[i] + b_mod[i], 3, axis=-1)
        shift_c, scale_c, gate_c = jnp.split(sctx @ w_mod[i] + b_mod[i], 3, axis=-1)
        hx = modulate(rms_norm(x, norm_g[i]), shift_x[:, None], scale_x[:, None])
        hc = modulate(rms_norm(ctx, norm_g[i]), shift_c, scale_c)
        j = i // 2
        if i % 2 == 0:
            ox, oc = diff_attention_mixer(hx, hc, a_w_in[j], a_lambda_q1[j], a_lambda_k1[j],
                                          a_lambda_q2[j], a_lambda_k2[j], a_subln_g[j], i,
                                          not last, ang_r, ang_c)
        else:
            ox, oc = window_gqa_mixer(hx, hc, b_w_in[j], b_sink[j], not last, ang_r, ang_c)
        x = x + gate_x[:, None] * (ox @ w_o[i])
        if not last:
            ctx = ctx + gate_c * (oc @ w_o[i])
    return rms_norm(x, final_g)
```

```python
import contextlib
import math

import numpy as np

import concourse.bass as bass
import concourse.mybir as mybir
from concourse.bass_utils import run_bass_kernel_spmd

F32 = mybir.dt.float32
BF = mybir.dt.bfloat16
AF = mybir.ActivationFunctionType
ALU = mybir.AluOpType
AX = mybir.AxisListType

N_CORES = 8
D = 1024
S = 2048
NCTX = 256
NT = 18
GRID_W = 64
NORM_EPS = 1e-6
SUBLN_EPS = 1e-5
NEG = -30000.0


class _Op:
    __slots__ = ("idx", "eng", "fn", "deps", "signal", "dma", "sem", "val")

    def __init__(self, idx, eng, fn, dma):
        self.idx = idx
        self.eng = eng
        self.fn = fn
        self.deps = set()
        self.signal = False
        self.dma = dma
        self.sem = None
        self.val = 0


class _Rec:
    def __init__(self):
        self.call = None

    def __getattr__(self, name):
        def f(*a, **k):
            assert self.call is None
            self.call = (name, a, k)
            return self
        return f


def _replay(call):
    return lambda eng: getattr(eng, call[0])(*call[1], **call[2])


class Prog:
    ENGINES = ("pe", "act", "dve", "pool", "sp")
    DMA_POOL = {"sp": 12, "pool": 8}

    def __init__(self, nc):
        self.nc = nc
        self.ops = []
        self.by_eng = {e: [] for e in self.ENGINES}
        self.last_writer = {}
        self.readers = {}
        self.open_dmas = []

    def add(self, eng, fn, reads=(), writes=(), dma=False):
        rec = _Rec()
        fn(rec)
        op = _Op(len(self.ops), eng, _replay(rec.call), dma)
        ps_r = [k for k in reads if isinstance(k, tuple) and k[0] == "ps"]
        if ps_r:
            reads = [k for k in reads if k not in ps_r]
            writes = list(writes) + ps_r
        deps = set()
        for k in reads:
            w = self.last_writer.get(k)
            if w is not None:
                deps.add(w)
        for k in writes:
            w = self.last_writer.get(k)
            if w is not None:
                deps.add(w)
            for r in self.readers.get(k, ()):
                deps.add(r)
        for k in reads:
            self.readers.setdefault(k, []).append(op.idx)
        for k in writes:
            self.last_writer[k] = op.idx
            self.readers[k] = []
        for d in deps:
            p = self.ops[d]
            if eng == "pe" and p.eng == "pe" and not p.dma and not dma:
                continue
            op.deps.add(d)
            p.signal = True
        self.ops.append(op)
        self.by_eng[eng].append(op)
        if dma:
            self.open_dmas.append(op.idx)
        return op.idx

    def wait_for(self, eng, idxs):
        op = _Op(len(self.ops), eng, None, False)
        for d in idxs:
            op.deps.add(d)
            self.ops[d].signal = True
        self.ops.append(op)
        self.by_eng[eng].append(op)

    def barrier(self):
        last = []
        for e in self.ENGINES:
            for op in reversed(self.by_eng[e]):
                if op.fn is not None and not op.dma:
                    last.append(op.idx)
                    break
        last += self.open_dmas
        self.open_dmas = []
        for e in self.ENGINES:
            self.wait_for(e, last)

    def emit(self):
        nc = self.nc
        with contextlib.ExitStack() as st:
            sems = {e: st.enter_context(nc.semaphore("s_" + e)) for e in ("pe", "act", "dve", "pool")}
            dsems = {e: [st.enter_context(nc.semaphore("d_%s%d" % (e, i))) for i in range(n)]
                     for e, n in self.DMA_POOL.items()}
            cnt = {e: 0 for e in self.ENGINES}
            dcnt = {e: 0 for e in self.ENGINES}
            hist = {e: [] for e in self.ENGINES}
            for e in self.ENGINES:
                for op in self.by_eng[e]:
                    if op.fn is None:
                        continue
                    if op.dma:
                        pool = dsems[e]
                        j = dcnt[e]
                        dcnt[e] += 1
                        op.sem = pool[j % len(pool)]
                        op.val = 16 * (j // len(pool) + 1)
                        op.signal = True
                        if j >= len(pool):
                            op.deps.add(hist[e][j - len(pool)])
                        hist[e].append(op.idx)
                    elif op.signal:
                        cnt[e] += 1
                        op.sem = sems[e]
                        op.val = cnt[e]
            block = st.enter_context(nc.Block())

            def run(e):
                def body(eng):
                    waited = {}
                    for op in self.by_eng[e]:
                        need = {}
                        for d in op.deps:
                            p = self.ops[d]
                            if need.get(p.sem.num, (None, 0))[1] < p.val:
                                need[p.sem.num] = (p.sem, p.val)
                        for key, (s, v) in need.items():
                            if waited.get(key, 0) < v:
                                eng.wait_ge(s, v)
                                waited[key] = v
                        if op.fn is None:
                            continue
                        ins = op.fn(eng)
                        if op.signal:
                            ins.then_inc(op.sem, 16 if op.dma else 1)
                return body

            block.tensor(run("pe"))
            block.scalar(run("act"))
            block.vector(run("dve"))
            block.gpsimd(run("pool"))
            block.sync(run("sp"))


def _ap(base, off, dims):
    return bass.AP(base.tensor, base.offset + off, [list(base.ap[0])] + [list(d) for d in dims])


class _Stop(Exception):
    pass


STATS = {}
QK256 = True
LOOKAHEAD = 2


def build(layers=(0, 1), nb=4, final=True, stop=None):
    nc = bass.Bass("TRN2", target_bir_lowering=False)

    def din(name, shape):
        return nc.dram_tensor(name, shape, F32, kind="ExternalInput").ap()

    x_d = din("x", [nb, S, D])
    ctx_d = din("ctx", [nb, NCTX, D])
    c_d = din("c", [nb, D])
    cctx_d = din("c_ctx", [1, D])
    wmod_d = din("w_mod", [2, D, 3 * D])
    bmod_d = din("b_mod", [2, 3 * D])
    ng_d = din("norm_g", [2, D])
    wo_d = din("w_o", [2, D, D])
    win_d = [din("a_w_in", [D, 4096]), din("b_w_in", [D, 2560])]
    lam_d = din("a_lam", [4, 64])
    sg_d = din("a_subln_g", [1, 128])
    sink_d = din("b_sink", [1, 16])
    fg_d = din("final_g", [1, D])
    ident_d = din("k_ident", [128, 128])
    cos_d = din("k_cos", [128, 16, 64])
    sin_d = din("k_sin", [128, 16, 64])
    mask_d = din("k_mask", [128, 2, 128])
    y_d = nc.dram_tensor("y", [nb, S, D], F32, kind="ExternalOutput").ap()
    yc_d = None
    if not final:
        yc_d = nc.dram_tensor("yc", [nb, NCTX, D], F32, kind="ExternalOutput").ap()
    scr_d = nc.dram_tensor("scr_gate", [2, 5, D], F32, kind="Internal").ap()
    dbg_d = nc.dram_tensor("dbg", [128, 24576], F32, kind="ExternalOutput").ap() if stop else None

    P = Prog(nc)
    add = P.add

    def bcast_rows(src2d, row, nparts, n):
        a = src2d[row:row + 1, 0:n]
        return bass.AP(a.tensor, a.offset, [[0, nparts], [1, n]])

    with contextlib.ExitStack() as st:
        def sb(name, shape, dt):
            return st.enter_context(nc.sbuf_tensor(name, shape, dt))

        idf = sb("idf", [128, 128], F32)
        idb = sb("idb", [128, 128], BF)
        cosT = sb("cosT", [128, 16, 64], F32)
        sinT = sb("sinT", [128, 16, 64], F32)
        maskb = sb("maskb", [128, 2, 512], BF)
        ABcol = sb("ABcol", [128, 2, 2, 8, 5], F32)
        mhalf = sb("mhalf", [128, 1], F32)
        neglam = sb("neglam", [128, 1], F32)
        esl = sb("esl", [128, 16], F32)
        mB = sb("mB", [128, 128], F32)
        gB2 = sb("gB2", [128, D], F32)
        psA = st.enter_context(nc.psum_tensor("psA", [128, 4, 512], F32))
        psB = st.enter_context(nc.psum_tensor("psB", [128, 4, 512], F32))

        def bank(i):
            return (psA if i < 4 else psB)[:, i % 4, :]

        R = sb("R", [128, NT, D], F32)
        hT = sb("hT", [128, 8, NT * 128], BF)
        KT = sb("KT", [128, 2, NT * 128], BF)
        V = sb("V", [128, NT, 2, 130], BF)
        QT = [sb("QT%d" % i, [128, 2, 512], BF) for i in range(2)]
        GS = [sb("GS%d" % i, [128, 4, 256], BF) for i in range(3)]
        OG = GS
        ET = [sb("ET%d" % i, [128, 2, 512], BF) for i in range(3)]
        Wkv = sb("Wkv", [128, 8, 512], BF)
        Wqg = sb("Wqg", [128, 8, 512], BF)
        Wo = sb("Wo", [128, 2, D], BF)
        gateB = sb("gateB", [128, D], F32)
        big = [sb("big%d" % i, [128, D], F32) for i in range(2)]
        stat = sb("stat", [128, 4, NT], F32)
        ta = [sb("ta%d" % i, [128, 256], F32) for i in range(2)]
        tb = [sb("tb%d" % i, [128, 256], F32) for i in range(2)]
        rot = [sb("rot%d" % i, [128, 256], BF) for i in range(4)]
        tg = [sb("tg%d" % i, [128, 256], F32) for i in range(2)]
        ug = tg
        e0 = sb("e0", [128, 512], F32)
        e1 = sb("e1", [128, 512], F32)
        e2 = sb("e2", [128, 512], F32)
        est = sb("est", [128, 32], F32)
        OGT = [sb("OGT%d" % i, [128, 2, 128], BF) for i in range(2)]
        junk = sb("junk", [128, D], BF)
        sT = sb("sT", [128, 8, 5], F32)
        lq = tb[0][:].rearrange("p (a d) -> p a d", d=64)
        lt = sb("lt", [128, 8], F32)
        sgB = tb[1][:, 0:128]
        es16 = tb[1][:, 128:144]
        maskf = ta[0][:].rearrange("p (a d) -> p a d", d=128)

        def f32view(t):
            return t[:].rearrange("p a b -> p (a b)").bitcast(F32)

        crow = big[0][0:5, :]
        erow = big[1][0:5, :]
        wch = [R[:, 4 * i:4 * i + 4, :].rearrange("p t (a n) -> p (t a) n", n=512) for i in range(2)]
        rch = [e0[0:5, :], e1[0:5, :]]
        bch = [f32view(GS[i])[0:5, :] for i in range(2)]
        gch = [f32view(ET[i])[0:5, :] for i in range(2)]
        ach = [e2[0:5, :], f32view(QT[0])[0:5, :]]

        add("pool", lambda e: e.memset(big[0][0:32, :], 0.0), writes=["crow"])
        add("pool", lambda e: e.memset(mhalf[:], -0.5), writes=["mhalf"])
        add("sp", lambda e: e.dma_start(out=idf[:], in_=ident_d), writes=["idf"], dma=True)
        add("sp", lambda e: e.dma_start(out=big[0][0:nb, :], in_=c_d), reads=["crow"], writes=["crow"], dma=True)
        add("sp", lambda e: e.dma_start(out=big[0][4:5, :], in_=cctx_d), reads=["crow"], writes=["crow4"], dma=True)
        add("sp", lambda e: e.dma_start(out=cosT[:], in_=cos_d), writes=["cos"], dma=True)
        add("sp", lambda e: e.dma_start(out=sinT[:], in_=sin_d), writes=["sin"], dma=True)
        add("sp", lambda e: e.dma_start(out=maskf, in_=mask_d), writes=["maskf"], dma=True)
        for i in range(4):
            add("sp", lambda e, i=i: e.dma_start(out=lq[:, i, :], in_=bcast_rows(lam_d, i, 128, 64)), writes=[("lq", i)], dma=True)
        add("sp", lambda e: e.dma_start(out=sgB, in_=bcast_rows(sg_d, 0, 128, 128)), writes=["sgB"], dma=True)
        add("sp", lambda e: e.dma_start(out=es16, in_=bcast_rows(sink_d, 0, 128, 16)), writes=["es16"], dma=True)
        add("dve", lambda e: e.tensor_copy(out=idb[:], in_=idf[:]), reads=["idf"], writes=["idb"])
        add("dve", lambda e: e.tensor_copy(out=maskb[:].rearrange("p t (s q) -> p t s q", s=4),
                                           in_=_ap(maskf[:, 0, :], 0, [[128, 2], [0, 4], [1, 128]])), reads=["maskf"], writes=["maskb"])
        add("act", lambda e: e.activation(out=erow, in_=crow, func=AF.Exp, scale=-1.0), reads=["crow", "crow4"], writes=["erow"])
        add("dve", lambda e: e.tensor_scalar(out=erow, in0=erow, scalar1=1.0, scalar2=None, op0=ALU.add), reads=["erow"], writes=["erow"])
        add("dve", lambda e: e.reciprocal(out=erow, in_=erow), reads=["erow"], writes=["erow"])
        add("dve", lambda e: e.tensor_tensor(out=erow, in0=erow, in1=crow, op=ALU.mult), reads=["erow", "crow", "crow4"], writes=["erow"])
        for kc in range(8):
            add("pe", lambda e, kc=kc: e.transpose(out=bank(0)[:, kc * 5:kc * 5 + 5], in_=big[1][0:5, kc * 128:(kc + 1) * 128], identity=idf[0:5, 0:5]),
                reads=["erow", "idf"], writes=[("ps", 0)])
        add("dve", lambda e: e.tensor_copy(out=sT[:].rearrange("p a b -> p (a b)"), in_=bank(0)[:, 0:40]), reads=[("ps", 0)], writes=["sT"])
        ci = 0
        for l in range(2):
            for cc in range(6):
                wb_ = ci % 2
                bk = 1 + ci % 2
                kind = cc // 2
                add("sp", lambda e, l=l, cc=cc, wb_=wb_: e.dma_start(
                    out=wch[wb_], in_=wmod_d[l, :, cc * 512:(cc + 1) * 512].rearrange("(kc p) n -> p kc n", p=128)),
                    writes=[("wch", wb_)], dma=True)
                add("sp", lambda e, l=l, cc=cc, wb_=wb_: e.dma_start(out=bch[wb_], in_=_ap(bmod_d[l:l + 1, cc * 512:(cc + 1) * 512], 0, [[1, 512]])
                                                                      if False else bass.AP(bmod_d.tensor, l * 3 * D + cc * 512, [[0, 5], [1, 512]])),
                    writes=[("bch", wb_)], dma=True)
                if kind == 1:
                    add("sp", lambda e, l=l, cc=cc, wb_=wb_: e.dma_start(out=gch[wb_], in_=bass.AP(ng_d.tensor, l * D + (cc - 2) * 512, [[0, 5], [1, 512]])),
                        writes=[("gch", wb_)], dma=True)
                for kc in range(8):
                    add("pe", lambda e, kc=kc, wb_=wb_, bk=bk: e.matmul(bank(bk)[0:5, :], lhsT=sT[:, kc, :], rhs=wch[wb_][:, kc, :],
                                                                         start=(kc == 0), stop=(kc == 7)),
                        reads=["sT", ("wch", wb_)], writes=[("ps", bk)])
                add("dve", lambda e, wb_=wb_, bk=bk: e.tensor_tensor(out=rch[wb_], in0=bank(bk)[0:5, :], in1=bch[wb_], op=ALU.add),
                    reads=[("ps", bk), ("bch", wb_)], writes=[("rch", wb_)])
                if kind == 2:
                    add("sp", lambda e, l=l, cc=cc, wb_=wb_: e.dma_start(out=scr_d[l, :, (cc - 4) * 512:(cc - 3) * 512], in_=rch[wb_]),
                        reads=[("rch", wb_)], writes=[("scr", l)], dma=True)
                    ci += 1
                    continue
                src = rch[wb_]
                skey = ("rch", wb_)
                if kind == 1:
                    add("dve", lambda e, wb_=wb_: e.scalar_tensor_tensor(out=ach[wb_], in0=rch[wb_], scalar=1.0, in1=gch[wb_], op0=ALU.add, op1=ALU.mult),
                        reads=[("rch", wb_), ("gch", wb_)], writes=[("ach", wb_)])
                    src = ach[wb_]
                    skey = ("ach", wb_)
                ab = 0 if kind == 1 else 1
                for k4 in range(4):
                    kc = (cc % 2) * 4 + k4
                    col = ((l * 2 + ab) * 8 + kc) * 5
                    add("pe", lambda e, src=src, col=col, k4=k4: e.transpose(out=bank(3)[:, col:col + 5], in_=src[:, k4 * 128:(k4 + 1) * 128],
                                                                            identity=idf[0:5, 0:5]),
                        reads=[skey, "idf"], writes=[("ps", 3)])
                ci += 1
        add("dve", lambda e: e.tensor_copy(out=ABcol[:].rearrange("p a b c d -> p (a b c d)"), in_=bank(3)[:, 0:160]), reads=[("ps", 3)], writes=["ABcol"])
        lam_init = 0.8 - 0.6 * math.exp(-0.3 * 0)
        add("dve", lambda e: e.tensor_tensor(out=lq[:, 0, :], in0=lq[:, 0, :], in1=lq[:, 1, :], op=ALU.mult), reads=[("lq", 0), ("lq", 1)], writes=[("lq", 0)])
        add("dve", lambda e: e.tensor_tensor(out=lq[:, 2, :], in0=lq[:, 2, :], in1=lq[:, 3, :], op=ALU.mult), reads=[("lq", 2), ("lq", 3)], writes=[("lq", 2)])
        add("dve", lambda e: e.tensor_reduce(out=lt[:, 0:1], in_=lq[:, 0, :], axis=AX.X, op=ALU.add), reads=[("lq", 0)], writes=["lt0"])
        add("dve", lambda e: e.tensor_reduce(out=lt[:, 1:2], in_=lq[:, 2, :], axis=AX.X, op=ALU.add), reads=[("lq", 2)], writes=["lt1"])
        add("act", lambda e: e.activation(out=lt[:, 2:4], in_=lt[:, 0:2], func=AF.Exp), reads=["lt0", "lt1"], writes=["lt2"])
        add("dve", lambda e: e.tensor_tensor(out=lt[:, 4:5], in0=lt[:, 3:4], in1=lt[:, 2:3], op=ALU.subtract), reads=["lt2"], writes=["lt4"])
        add("dve", lambda e: e.tensor_scalar(out=neglam[:], in0=lt[:, 4:5], scalar1=-lam_init, scalar2=None, op0=ALU.add), reads=["lt4"], writes=["neglam"])
        add("act", lambda e: e.activation(out=es16, in_=es16, func=AF.Exp), reads=["es16"], writes=["es16"])
        add("dve", lambda e: e.tensor_copy(out=esl[:].rearrange("p (g h b) -> p g h b", g=4, h=2), in_=_ap(es16, 0, [[4, 4], [1, 2], [2, 2]])),
            reads=["es16"], writes=["esl"])
        sgc = 0.5 * (1.0 - lam_init)
        add("dve", lambda e: e.tensor_scalar(out=mB[:], in0=sgB, scalar1=sgc, scalar2=None, op0=ALU.mult), reads=["sgB"], writes=["mB"])
        P.barrier()

        ctr = {}

        def nxt(name, n):
            v = ctr.get(name, 0)
            ctr[name] = v + 1
            return v % n

        gp_set = [[0, 1, 2, 3]]

        def gp():
            st_ = gp_set[0]
            return st_[nxt("gp", len(st_))]

        def rstd(tag, col, n_inv, eps, src_keys):
            add("dve", lambda e: e.tensor_scalar(out=stat[:, 2, col:col + 1], in0=stat[:, 0, col:col + 1], scalar1=n_inv, scalar2=eps,
                                                 op0=ALU.mult, op1=ALU.add), reads=src_keys, writes=[("st2", col)])
            add("pool", lambda e: e.tensor_tensor(out=stat[:, 1, col:col + 1], in0=stat[:, 2, col:col + 1], in1=mhalf[:], op=ALU.pow),
                reads=[("st2", col), "mhalf"], writes=[("st1", col)])

        def rope(src, width, ltile, out_ap, out_keys, src_keys):
            hm = width // 64
            i = nxt("rope", 2)
            add("dve", lambda e: e.tensor_tensor(out=ta[i][:, 0:width].rearrange("p (h d) -> p h d", d=64), in0=src.rearrange("p (h d) -> p h d", d=64),
                                                 in1=_ap(cosT[:, ltile, :], 0, [[0, hm], [1, 64]]), op=ALU.mult),
                reads=src_keys + ["cos"], writes=[("ta", i)])
            for rc in range(2):
                add("dve", lambda e, rc=rc: e.tensor_tensor(
                    out=_ap(tb[i][:, 0:width], rc * 32, [[64, hm], [16, 2], [1, 16]]),
                    in0=_ap(src, rc * 32 + 16, [[64, hm], [-16, 2], [1, 16]]),
                    in1=_ap(sinT[:, ltile, :], rc * 32, [[0, hm], [16, 2], [1, 16]]), op=ALU.mult),
                    reads=src_keys + ["sin"], writes=[("tb", i, rc)])
            add("pool", lambda e: e.tensor_tensor(out=out_ap, in0=ta[i][:, 0:width], in1=tb[i][:, 0:width], op=ALU.add),
                reads=[("ta", i), ("tb", i, 0), ("tb", i, 1)], writes=out_keys)

        def wcols(l, g):
            if l == 0:
                return g * 256, 1024 + g * 256, 2048 + g * 256, 3072 + g * 256, 256
            return g * 256, 1024 + g * 64, 1280 + g * 64, 1536 + g * 256, 64

        def wld(l, dst, c0, n):
            w = win_d[l]
            return lambda e: e.dma_start(out=dst, in_=w[:, c0:c0 + n].rearrange("(kc p) n -> p kc n", p=128))

        def load_kv(l, g):
            qc, kc_, vc, gc, kw = wcols(l, g)
            add("pool", wld(l, Wkv[:, :, 0:kw], kc_, kw), writes=[("Wkv", 0)], dma=True)
            add("pool", wld(l, Wkv[:, :, kw:2 * kw], vc, kw), writes=[("Wkv", 1)], dma=True)

        def load_qg(l, g):
            qc, kc_, vc, gc, kw = wcols(l, g)
            add("pool", wld(l, Wqg[:, :, 0:256], qc, 256), writes=[("Wqg", 0)], dma=True)
            add("pool", wld(l, Wqg[:, :, 256:512], gc, 256), writes=[("Wqg", 1)], dma=True)

        def load_wo(l, g):
            add("pool", lambda e: e.dma_start(out=Wo[:], in_=wo_d[l, g * 256:(g + 1) * 256, :].rearrange("(kc p) n -> p kc n", p=128)),
                writes=["Wo"], dma=True)

        def phase0(l, b):
            xbs = {}

            def s1(tt):
                add("act", lambda e: e.activation(out=junk[:], in_=R[:, tt, :], func=AF.Square, accum_out=stat[:, 0, tt:tt + 1]),
                    reads=[("R", tt)], writes=[("st0", tt)])
                rstd("n", tt, 1.0 / D, NORM_EPS, [("st0", tt)])

            def s3(tt):
                xb = nxt("big", 2)
                xbs[tt] = xb
                add("pool", lambda e: e.tensor_tensor(out=big[xb][:], in0=R[:, tt, :], in1=_ap(stat[:, 1, tt:tt + 1], 0, [[0, D]]), op=ALU.mult),
                    reads=[("R", tt), ("st1", tt)], writes=[("big", xb)])

            def s4(tt):
                bb = 4 if tt < 2 else b
                xb = xbs[tt]
                for half in range(2):
                    bk = gp()
                    for k4 in range(4):
                        kc = half * 4 + k4
                        add("pe", lambda e: e.transpose(out=bank(bk)[:, k4 * 128:(k4 + 1) * 128],
                                                        in_=big[xb][:, kc * 128:(kc + 1) * 128], identity=idf[:]),
                            reads=[("big", xb), "idf"], writes=[("ps", bk)])
                    for k4 in range(4):
                        kc = half * 4 + k4
                        dst = hT[:, kc, tt * 128:(tt + 1) * 128]
                        src = bank(bk)[:, k4 * 128:(k4 + 1) * 128]
                        a_ = ABcol[:, l, 0, kc, bb:bb + 1]
                        b_ = ABcol[:, l, 1, kc, bb:bb + 1]
                        if k4 == 0:
                            add("act", lambda e: e.activation(out=dst, in_=src, func=AF.Identity, scale=a_, bias=b_),
                                reads=[("ps", bk), "ABcol"], writes=[("hT", tt, kc)])
                        else:
                            add("dve", lambda e: e.tensor_scalar(out=dst, in0=src, scalar1=a_, scalar2=b_, op0=ALU.mult, op1=ALU.add),
                                reads=[("ps", bk), "ABcol"], writes=[("hT", tt, kc)])

            for i in range(NT + 2):
                if i < NT:
                    s1(i)
                if 0 <= i - 1 < NT:
                    s3(i - 1)
                if 0 <= i - 2 < NT:
                    s4(i - 2)

        def hT_keys(tt):
            return [("hT", tt, kc) for kc in range(8)]

        def phase1(l, g):
            kw = 256 if l == 0 else 64
            nblk = 2 if l == 0 else 1
            vw = 128 if l == 0 else 64
            nv = 2 if l == 0 else 1
            ris = {}

            def mm(tt):
                bk = gp()
                for kc in range(8):
                    add("pe", lambda e: e.matmul(bank(bk)[:, 0:2 * kw], lhsT=hT[:, kc, tt * 128:(tt + 1) * 128],
                                                 rhs=Wkv[:, kc, 0:2 * kw], start=(kc == 0), stop=(kc == 7)),
                        reads=[("hT", tt, kc), ("Wkv", 0), ("Wkv", 1)], writes=[("ps", bk)])
                add("act", lambda e: e.activation(out=V[:, tt, 0:nv, 0:vw], in_=bank(bk)[:, kw:2 * kw].rearrange("p (h d) -> p h d", h=nv),
                                                  func=AF.Copy),
                    reads=[("ps", bk)], writes=[("V", tt)])
                ri = nxt("rot", 4)
                ris[tt] = ri
                if tt < 2:
                    add("dve", lambda e: e.tensor_copy(out=rot[ri][:, 0:kw], in_=bank(bk)[:, 0:kw]), reads=[("ps", bk)], writes=[("rot", ri)])
                else:
                    rope(bank(bk)[:, 0:kw], kw, tt - 2, rot[ri][:, 0:kw], [("rot", ri)], [("ps", bk)])
                if l == 1:
                    add("pool", lambda e: e.tensor_copy(out=rot[ri][:, 64:128], in_=rot[ri][:, 0:64]), reads=[("rot", ri)], writes=[("rotd", ri)])

            def tr(tt):
                ri = ris[tt]
                bk2 = gp()
                pv = bank(bk2).bitcast(BF)
                for blk in range(nblk):
                    add("pe", lambda e: e.transpose(out=pv[:, blk * 128:(blk + 1) * 128], in_=rot[ri][:, blk * 128:(blk + 1) * 128], identity=idb[:]),
                        reads=[("rot", ri), ("rotd", ri), "idb"], writes=[("ps", bk2)])
                add("act", lambda e: e.activation(out=KT[:, 0:nblk, tt * 128:(tt + 1) * 128],
                                                  in_=pv[:, 0:nblk * 128].rearrange("p (a b) -> p a b", a=nblk), func=AF.Copy),
                    reads=[("ps", bk2)], writes=[("KT", tt)])

            LAG = 2
            for i in range(NT + LAG):
                if i < NT:
                    mm(i)
                if 0 <= i - LAG < NT:
                    tr(i - LAG)

        def phase2a_mm(l, g, cb, q_tiles):
            ris = []
            for j, tt in enumerate(q_tiles):
                bk = gp()
                for kc in range(8):
                    add("pe", lambda e, tt=tt, kc=kc, bk=bk: e.matmul(bank(bk)[:, :], lhsT=hT[:, kc, tt * 128:(tt + 1) * 128],
                                                                       rhs=Wqg[:, kc, :], start=(kc == 0), stop=(kc == 7)),
                        reads=[("hT", tt, kc), ("Wqg", 0), ("Wqg", 1)], writes=[("ps", bk)])
                gi = nxt("tg", 2)
                add("act", lambda e, bk=bk, gi=gi: e.activation(out=tg[gi][:], in_=bank(bk)[:, 256:512], func=AF.Tanh, scale=0.5),
                    reads=[("ps", bk)], writes=[("tg", gi)])
                if l == 0:
                    add("dve", lambda e, bk=bk, gi=gi: e.scalar_tensor_tensor(out=tg[gi][:], in0=tg[gi][:], scalar=1.0, in1=bank(bk)[:, 256:512],
                                                                              op0=ALU.add, op1=ALU.mult),
                        reads=[("ps", bk), ("tg", gi)], writes=[("tg", gi)])
                    add("pool", lambda e, gi=gi, j=j: e.tensor_tensor(out=GS[cb][:, j, :].rearrange("p (h d) -> p h d", h=2),
                                                                      in0=tg[gi][:].rearrange("p (h d) -> p h d", h=2),
                                                                      in1=_ap(mB[:], 0, [[0, 2], [1, 128]]), op=ALU.mult),
                        reads=[("tg", gi), "mB"], writes=[("GS", cb, j, 0), ("GS", cb, j, 1)])
                else:
                    add("dve", lambda e, bk=bk, gi=gi: e.scalar_tensor_tensor(out=tg[gi][:], in0=tg[gi][:], scalar=1.0, in1=bank(bk)[:, 256:512],
                                                                              op0=ALU.add, op1=ALU.mult),
                        reads=[("ps", bk), ("tg", gi)], writes=[("tg", gi)])
                    add("pool", lambda e, gi=gi, j=j: e.tensor_copy(out=GS[cb][:, j, :], in_=tg[gi][:]),
                        reads=[("tg", gi)], writes=[("GS", cb, j, 0), ("GS", cb, j, 1)])
                ri = nxt("rot", 4)
                ris.append(ri)
                if tt < 2:
                    add("dve", lambda e, bk=bk, ri=ri: e.tensor_copy(out=rot[ri][:], in_=bank(bk)[:, 0:256]), reads=[("ps", bk)], writes=[("rot", ri)])
                else:
                    rope(bank(bk)[:, 0:256], 256, tt - 2, rot[ri][:], [("rot", ri)], [("ps", bk)])
            return ris

        def phase2a_tr(l, g, qb, q_tiles, ris):
            for j, tt in enumerate(q_tiles):
                ri = ris[j]
                bk2 = gp()
                pv = bank(bk2).bitcast(BF)
                for blk in range(2):
                    add("pe", lambda e, ri=ri, blk=blk, pv=pv: e.transpose(out=pv[:, blk * 128:(blk + 1) * 128], in_=rot[ri][:, blk * 128:(blk + 1) * 128],
                                                                           identity=idb[:]),
                        reads=[("rot", ri), "idb"], writes=[("ps", bk2)])
                add("act", lambda e, j=j, pv=pv: e.activation(out=QT[qb][:, :, j * 128:(j + 1) * 128],
                                                              in_=pv[:, 0:256].rearrange("p (a b) -> p a b", a=2), func=AF.Copy),
                    reads=[("ps", bk2)], writes=[("QT", qb, j)])

        def attn0(cb, qb, q_tiles, key_tiles, hook=None):
            J = len(q_tiles)
            NQ = J * 128
            qkeys = [("QT", qb, j) for j in range(J)]
            nk = len(key_tiles)

            def qk(hh, ki):
                kt = key_tiles[ki]
                sbuf_i = nxt("sbuf", 2)
                for m in range(2):
                    add("pe", lambda e: e.matmul(
                        psA[:, 2 * sbuf_i + m, 0:NQ], lhsT=KT[m * 64:(m + 1) * 64, hh, kt * 128:(kt + 1) * 128],
                        rhs=QT[qb][m * 64:(m + 1) * 64, hh, 0:NQ], start=True, stop=True),
                        reads=[("KT", kt)] + qkeys, writes=[("ps", 2 * sbuf_i + m)])
                eb = nxt("et", 3)
                add("act", lambda e: e.activation(out=ET[eb][:, :, 0:NQ], in_=psA[:, 2 * sbuf_i:2 * sbuf_i + 2, 0:NQ],
                                                  func=AF.Exp, scale=0.125),
                    reads=[("ps", 2 * sbuf_i), ("ps", 2 * sbuf_i + 1)], writes=[("ET", eb)])
                return (hh, ki, eb)

            def pv(u):
                hh, ki, eb = u
                kt = key_tiles[ki]
                for j in range(J):
                    for m in range(2):
                        add("pe", lambda e: e.matmul(
                            psB[:, j, m * 256:m * 256 + 130], lhsT=ET[eb][:, m, j * 128:(j + 1) * 128], rhs=V[:, kt, hh, :],
                            start=(ki == 0 and m == 0), stop=(ki == nk - 1), skip_group_check=True),
                            reads=[("ET", eb), ("V", kt)], writes=[("ps", 4 + j)])
                if ki == nk - 1:
                    epi(hh)

            def epi(hh):
                okeys = [("ps", 4 + j) for j in range(J)]
                zv = _ap(psB[:, 0, :], 128, [[512, J], [256, 2]])
                rz = est[:, 0:2 * J].rearrange("p (j m) -> p j m", m=2)
                add("dve", lambda e, zv=zv, rz=rz: e.reciprocal(out=rz, in_=zv), reads=okeys, writes=["rz"])
                add("dve", lambda e: e.tensor_scalar(out=_ap(est[:, 0:1], 1, [[2, J]]), in0=_ap(est[:, 0:1], 1, [[2, J]]), scalar1=neglam[:], scalar2=None,
                                                     op0=ALU.mult), reads=["rz", "neglam"], writes=["rz"])
                for m, dst in ((0, e0), (1, e1)):
                    add("dve", lambda e, m=m, dst=dst: e.tensor_tensor(out=dst[:, 0:NQ].rearrange("p (j d) -> p j d", d=128),
                                                                       in0=_ap(psB[:, 0, :], m * 256, [[512, J], [1, 128]]),
                                                                       in1=_ap(est[:, 0:1], m, [[2, J], [0, 128]]), op=ALU.mult),
                        reads=okeys + ["rz"], writes=[("e", m)])
                add("dve", lambda e: e.tensor_tensor(out=e2[:, 0:NQ], in0=e0[:, 0:NQ], in1=e1[:, 0:NQ], op=ALU.add),
                    reads=[("e", 0), ("e", 1)], writes=[("e", 2)])
                add("dve", lambda e: e.tensor_tensor(out=e0[:, 0:NQ], in0=e2[:, 0:NQ], in1=e2[:, 0:NQ], op=ALU.mult), reads=[("e", 2)], writes=[("e", 0)])
                add("dve", lambda e: e.tensor_reduce(out=est[:, 8:8 + J], in_=e0[:, 0:NQ].rearrange("p (j d) -> p j d", d=128), axis=AX.X, op=ALU.add),
                    reads=[("e", 0)], writes=["ssq"])
                add("dve", lambda e: e.tensor_scalar(out=est[:, 12:12 + J], in0=est[:, 8:8 + J], scalar1=1.0 / 128, scalar2=SUBLN_EPS,
                                                     op0=ALU.mult, op1=ALU.add), reads=["ssq"], writes=["ssq2"])
                add("pool", lambda e: e.tensor_tensor(out=est[:, 16:16 + J], in0=est[:, 12:12 + J], in1=_ap(mhalf[:], 0, [[0, J]]), op=ALU.pow),
                    reads=["ssq2", "mhalf"], writes=["srs"])
                add("dve", lambda e: e.tensor_tensor(out=e1[:, 0:NQ].rearrange("p (j d) -> p j d", d=128),
                                                     in0=e2[:, 0:NQ].rearrange("p (j d) -> p j d", d=128),
                                                     in1=_ap(est[:, 16:17], 0, [[1, J], [0, 128]]), op=ALU.mult),
                    reads=[("e", 2), "srs"], writes=[("e", 1)])
                add("dve", lambda e: e.tensor_tensor(out=OG[cb][:, 0:J, hh * 128:(hh + 1) * 128],
                                                             in0=e1[:, 0:NQ].rearrange("p (j d) -> p j d", d=128),
                                                             in1=GS[cb][:, 0:J, hh * 128:(hh + 1) * 128], op=ALU.mult),
                    reads=[("e", 1)] + [("GS", cb, j, hh) for j in range(J)], writes=[("GS", cb, j, hh) for j in range(J)])
                check("ep%d" % hh, [(est[:], ["rz", "ssq", "ssq2", "srs"]), (e2[:], [("e", 2)]), (e1[:], [("e", 1)]),
                                    (e0[:], [("e", 0)]),
                                    (GS[cb][:].rearrange("p a b -> p (a b)"), [("GS", cb, j, h) for j in range(J) for h in range(2)])])

            pend = []
            for hh in range(2):
                for ki in range(nk):
                    pend.append(qk(hh, ki))
                    if len(pend) > LOOKAHEAD:
                        pv(pend.pop(0))
                    if hook is not None and hh == 0 and ki == min(5, nk - 1):
                        hook()
            while pend:
                pv(pend.pop(0))

        def attn1(g, cb, qb, q_tiles, hook=None):
            units = []
            for j, tt in enumerate(q_tiles):
                i = tt - 2
                keys = [(0, None), (1, None)]
                if i > 0:
                    keys.append((tt - 1, 0))
                keys.append((tt, None))
                if i < 15:
                    keys.append((tt + 1, 1))
                ob = nxt("ob", 2)
                for ki, (kt, mk) in enumerate(keys):
                    units.append((j, ob, ki, kt, mk, len(keys)))

            def qk(u):
                j, ob, ki, kt, mk, nk = u
                sb_ = nxt("sbuf", 2)
                for half in range(2):
                    bk = 2 * sb_ + half
                    if mk is not None:
                        add("pe", lambda e: e.matmul(psA[:, bk, 0:256], lhsT=idb[:], rhs=maskb[:, mk, 0:256], start=True, stop=False,
                                                     skip_group_check=True),
                            reads=["idb", "maskb"], writes=[("ps", bk)])
                    if QK256:
                        add("pe", lambda e: e.matmul(
                            psA[:, bk, 0:256].rearrange("p (a b) -> p a b", a=2),
                            lhsT=KT[half * 64:(half + 1) * 64, 0, kt * 128:(kt + 1) * 128],
                            rhs=QT[qb][half * 64:(half + 1) * 64, :, j * 128:(j + 1) * 128],
                            start=(mk is None), stop=True, skip_group_check=True),
                            reads=[("KT", kt), ("QT", qb, j)], writes=[("ps", bk)])
                        continue
                    for blk in range(2):
                        add("pe", lambda e: e.matmul(
                            psA[:, bk, blk * 128:(blk + 1) * 128],
                            lhsT=KT[half * 64:(half + 1) * 64, 0, kt * 128:(kt + 1) * 128],
                            rhs=QT[qb][half * 64:(half + 1) * 64, blk, j * 128:(j + 1) * 128],
                            start=(blk == 0 and mk is None), stop=(blk == 1), skip_group_check=True),
                            reads=[("KT", kt), ("QT", qb, j)], writes=[("ps", bk)])
                eb = nxt("et", 3)
                add("act", lambda e: e.activation(out=ET[eb][:, :, 0:256], in_=psA[:, 2 * sb_:2 * sb_ + 2, 0:256], func=AF.Exp, scale=0.125),
                    reads=[("ps", 2 * sb_), ("ps", 2 * sb_ + 1)], writes=[("ET", eb)])
                return eb

            def pv(u, eb):
                j, ob, ki, kt, mk, nk = u
                for s_ in range(4):
                    add("pe", lambda e: e.matmul(
                        psB[:, ob, s_ * 66:s_ * 66 + 66], lhsT=ET[eb][:, s_ // 2, (s_ % 2) * 128:(s_ % 2 + 1) * 128], rhs=V[:, kt, 0, 0:66],
                        start=(ki == 0 and s_ == 0), stop=(ki == nk - 1), skip_group_check=True),
                        reads=[("ET", eb), ("V", kt)], writes=[("ps", 4 + ob)])
                if ki == nk - 1:
                    epi(j, ob)

            def epi(j, ob):
                okey = [("ps", 4 + ob)]
                zt = est[:, 20:24]
                add("dve", lambda e: e.tensor_tensor(out=zt, in0=_ap(psB[:, ob, :], 64, [[66, 4]]), in1=esl[:, g * 4:(g + 1) * 4], op=ALU.add),
                    reads=okey + ["esl"], writes=["zt"])
                add("dve", lambda e: e.tensor_scalar(out=zt, in0=zt, scalar1=2.0, scalar2=None, op0=ALU.mult), reads=["zt"], writes=["zt"])
                add("dve", lambda e: e.reciprocal(out=est[:, 24:28], in_=zt), reads=["zt"], writes=["rzt"])
                add("dve", lambda e: e.tensor_tensor(out=e0[:, 0:256].rearrange("p (s d) -> p s d", d=64),
                                                     in0=_ap(psB[:, ob, :], 0, [[66, 4], [1, 64]]),
                                                     in1=_ap(est[:, 24:25], 0, [[1, 4], [0, 64]]), op=ALU.mult),
                    reads=okey + ["rzt"], writes=[("e", 0)])
                add("dve", lambda e: e.tensor_tensor(out=OG[cb][:, j, :].rearrange("p (b h d) -> p b h d", b=2, h=2),
                                                     in0=_ap(e0[:, 0:1], 0, [[64, 2], [128, 2], [1, 64]]),
                                                     in1=GS[cb][:, j, :].rearrange("p (b h d) -> p b h d", b=2, h=2), op=ALU.mult),
                    reads=[("e", 0), ("GS", cb, j, 0), ("GS", cb, j, 1)], writes=[("GS", cb, j, 0), ("GS", cb, j, 1)])

            pend = []
            for ui, u in enumerate(units):
                pend.append((u, qk(u)))
                if len(pend) > LOOKAHEAD:
                    pv(*pend.pop(0))
                if hook is not None and ui == 4:
                    hook()
            while pend:
                pv(*pend.pop(0))

        def phase2c(l, g, cb, q_tiles, fin=None):
            ois = {}

            def tr(j):
                bk2 = gp()
                pv = bank(bk2).bitcast(BF)
                oi = nxt("ogt", 2)
                ois[j] = oi
                for blk in range(2):
                    add("pe", lambda e: e.transpose(out=pv[:, blk * 128:(blk + 1) * 128], in_=OG[cb][:, j, blk * 128:(blk + 1) * 128], identity=idb[:]),
                        reads=[("GS", cb, j, 0), ("GS", cb, j, 1), "idb"], writes=[("ps", bk2)])
                add("dve", lambda e: e.tensor_copy(out=OGT[oi][:].rearrange("p a b -> p (a b)"), in_=pv[:, 0:256]),
                    reads=[("ps", bk2)], writes=[("OGT", oi)])

            def op(j):
                tt = q_tiles[j]
                oi = ois[j]
                gtile = gB2 if tt < 2 else gateB
                gkey = "gB2" if tt < 2 else "gateB"
                xb = nxt("big", 2)
                for half in range(2):
                    bk = gp()
                    for k2 in range(2):
                        add("pe", lambda e: e.matmul(bank(bk)[:, :], lhsT=OGT[oi][:, k2, :], rhs=Wo[:, k2, half * 512:(half + 1) * 512],
                                                     start=(k2 == 0), stop=(k2 == 1)),
                            reads=[("OGT", oi), "Wo"], writes=[("ps", bk)])
                    add("dve", lambda e: e.tensor_tensor(out=big[xb][:, half * 512:(half + 1) * 512], in0=bank(bk)[:, :],
                                                         in1=gtile[:, half * 512:(half + 1) * 512], op=ALU.mult),
                        reads=[("ps", bk), gkey], writes=[("big", xb, half)])
                add("pool", lambda e: e.tensor_tensor(out=R[:, tt, :], in0=R[:, tt, :], in1=big[xb][:], op=ALU.add),
                    reads=[("R", tt), ("big", xb, 0), ("big", xb, 1)], writes=[("R", tt)])
                if fin is not None:
                    fin(tt)

            J = len(q_tiles)
            for i in range(J + 1):
                if i < J:
                    tr(i)
                if i >= 1:
                    op(i - 1)

        _orig_add = P.add

        def add(eng, fn, reads=(), writes=(), dma=False):
            def ex(ks):
                out = []
                for k in ks:
                    out.append(k)
                    if isinstance(k, tuple) and k[0] == "big" and len(k) == 2:
                        out += [("big", k[1], 0), ("big", k[1], 1)]
                return out
            return _orig_add(eng, fn, ex(list(reads)), ex(list(writes)), dma)

        out_dmas = []
        dbg_col = [0]

        def dump(ap2d, keys):
            n = ap2d.shape[1]
            c0 = dbg_col[0]
            dbg_col[0] += n
            out_dmas.append(add("pool", lambda e: e.dma_start(out=dbg_d[0:ap2d.shape[0], c0:c0 + n], in_=ap2d, allow_slow_non_contiguous=True), reads=keys, dma=True))
            print("dbg", c0, n, keys[:2])

        def check(tag, items):
            if stop == tag:
                for ap2d, keys in items:
                    dump(ap2d, keys)
                raise _Stop()

        try:
            check("pro", [(ABcol[:].rearrange("p a b c d -> p (a b c d)"), ["ABcol"]), (neglam[:], ["neglam"]), (esl[:], ["esl"]), (mB[:], ["mB"]),
                          (maskb[:].rearrange("p a b -> p (a b)"), ["maskb"]), (idb[:], ["idb"])])
            def load_x(bn, t4):
                add("sp", lambda e: e.dma_start(out=R[:, 2 + 4 * t4:6 + 4 * t4, :],
                                                in_=x_d[bn, t4 * 512:(t4 + 1) * 512, :].rearrange("(t p) d -> p t d", p=128)),
                    writes=[("R", 2 + 4 * t4 + k) for k in range(4)], dma=True)

            def load_ctx(bn):
                add("sp", lambda e: e.dma_start(out=R[:, 0:2, :], in_=ctx_d[bn].rearrange("(t p) d -> p t d", p=128)),
                    writes=[("R", 0), ("R", 1)], dma=True)

            fuse_final = final and (1 in layers)

            def final_tile(b, tt):
                nonlocal prefetched
                xb = nxt("big", 2)
                add("act", lambda e: e.activation(out=junk[:], in_=R[:, tt, :], func=AF.Square, accum_out=stat[:, 0, tt:tt + 1]),
                    reads=[("R", tt)], writes=[("st0", tt)])
                rstd("f", tt, 1.0 / D, NORM_EPS, [("st0", tt)])
                add("dve", lambda e: e.scalar_tensor_tensor(out=big[xb][:], in0=R[:, tt, :], scalar=stat[:, 1, tt:tt + 1], in1=gB2[:],
                                                            op0=ALU.mult, op1=ALU.mult),
                    reads=[("R", tt), ("st1", tt), "gB2"], writes=[("big", xb)])
                out_dmas.append(add("sp", lambda e: e.dma_start(out=y_d[b, (tt - 2) * 128:(tt - 1) * 128, :], in_=big[xb][:]),
                                    reads=[("big", xb)], dma=True))
                if b + 1 < nb and (tt - 2) % 4 == 3:
                    load_x(b + 1, (tt - 2) // 4)
                    prefetched = True

            prefetched = False
            for b in range(nb):
                if not prefetched:
                    for t4 in range(4):
                        load_x(b, t4)
                    load_ctx(b)
                for l in layers:
                    last = (l == 1)
                    add("sp", lambda e, l=l, b=b: e.dma_start(out=gateB[:], in_=bcast_rows(scr_d[l], b, 128, D)),
                        reads=[("scr", l)], writes=["gateB"], dma=True)
                    if not last:
                        add("sp", lambda e, l=l: e.dma_start(out=gB2[:], in_=bcast_rows(scr_d[l], 4, 128, D)),
                            reads=[("scr", l)], writes=["gB2"], dma=True)
                    ocol = 128 if l == 0 else 64
                    add("pool", lambda e, ocol=ocol: e.memset(V[:, :, :, ocol:ocol + 1], 1.0), writes=[("V", tt) for tt in range(NT)])
                    load_kv(l, 0)
                    load_qg(l, 0)
                    load_wo(l, 0)
                    phase0(l, b)
                    if fuse_final and last:
                        add("sp", lambda e: e.dma_start(out=gB2[:], in_=bcast_rows(fg_d, 0, 128, D)), writes=["gB2"], dma=True)
                        if b + 1 < nb:
                            load_ctx(b + 1)
                    check("p0", [(hT[:].rearrange("p a b -> p (a b)"), [k for tt in range(NT) for k in hT_keys(tt)])])
                    for g in range(4):
                        phase1(l, g)
                        check("p1", [(KT[:].rearrange("p a b -> p (a b)"), [("KT", tt) for tt in range(NT)]),
                                     (V[:].rearrange("p a b c -> p (a b c)"), [("V", tt) for tt in range(NT)])])
                        if g < 3:
                            load_kv(l, g + 1)
                        chunks = []
                        if not last:
                            chunks.append([0, 1])
                        for c4 in range(4):
                            chunks.append([2 + 4 * c4 + k for k in range(4)])
                        gp_set[0] = [6, 7] if l == 1 else [0, 1, 2, 3]
                        fin = None
                        if fuse_final and last and g == 3:
                            fin = (lambda tt, b=b: final_tile(b, tt))
                        cbs = [nxt("cb", 3) for _ in chunks]
                        qbs = [nxt("qb", 2) for _ in chunks]
                        ris0 = phase2a_mm(l, g, cbs[0], chunks[0])
                        phase2a_tr(l, g, qbs[0], chunks[0], ris0)
                        for ci, q_tiles in enumerate(chunks):
                            nxt_ris = None
                            if ci + 1 < len(chunks):
                                nxt_ris = phase2a_mm(l, g, cbs[ci + 1], chunks[ci + 1])
                            hook = None
                            if ci >= 1:
                                hook = (lambda pc=ci - 1: phase2c(l, g, cbs[pc], chunks[pc], fin))
                            if l == 0:
                                attn0(cbs[ci], qbs[ci], q_tiles, [0, 1] if q_tiles[0] < 2 else list(range(NT)), hook)
                            else:
                                attn1(g, cbs[ci], qbs[ci], q_tiles, hook)
                            if nxt_ris is not None:
                                phase2a_tr(l, g, qbs[ci + 1], chunks[ci + 1], nxt_ris)
                        phase2c(l, g, cbs[-1], chunks[-1], fin)
                        gp_set[0] = [0, 1, 2, 3]
                        if g < 3:
                            load_qg(l, g + 1)
                            load_wo(l, g + 1)
                if fuse_final:
                    pass
                else:
                    for t4 in range(4):
                        out_dmas.append(add("sp", lambda e, b=b, t4=t4: e.dma_start(
                            out=y_d[b, t4 * 512:(t4 + 1) * 512, :].rearrange("(t p) d -> p t d", p=128), in_=R[:, 2 + 4 * t4:6 + 4 * t4, :]),
                            reads=[("R", 2 + 4 * t4 + k) for k in range(4)], dma=True))
                    out_dmas.append(add("sp", lambda e, b=b: e.dma_start(out=yc_d[b].rearrange("(t p) d -> p t d", p=128), in_=R[:, 0:2, :]),
                                        reads=[("R", 0), ("R", 1)], dma=True))
        except _Stop:
            pass
        P.wait_for("pool", out_dmas)
        P.wait_for("sp", out_dmas)
        STATS.update({e: len(v) for e, v in P.by_eng.items()})
        STATS["P"] = P
        P.emit()
    return nc


def _consts():
    ident = np.eye(128, dtype=np.float32)
    tok = np.arange(S)
    row = (tok // GRID_W).astype(np.float32)
    col = (tok % GRID_W).astype(np.float32)
    inv = (10000.0 ** (-np.arange(0, 32, 2, dtype=np.float32) / 32.0)).astype(np.float32)
    ar = row[:, None] * inv[None, :]
    ac = col[:, None] * inv[None, :]
    cos64 = np.concatenate([np.cos(ar), np.cos(ar), np.cos(ac), np.cos(ac)], axis=1)
    sin64 = np.concatenate([-np.sin(ar), np.sin(ar), -np.sin(ac), np.sin(ac)], axis=1)
    cosT = np.ascontiguousarray(cos64.reshape(16, 128, 64).transpose(1, 0, 2)).astype(np.float32)
    sinT = np.ascontiguousarray(sin64.reshape(16, 128, 64).transpose(1, 0, 2)).astype(np.float32)
    k = np.arange(128)[:, None]
    q = np.arange(128)[None, :]
    mask = np.zeros((128, 2, 128), np.float32)
    mask[:, 0, :] = np.where(k >= q, 0.0, NEG)
    mask[:, 1, :] = np.where(k <= q, 0.0, NEG)
    return dict(k_ident=ident, k_cos=cosT, k_sin=sinT, k_mask=mask)


_CACHE = {}


def _program(layers, nb, final):
    key = (tuple(layers), nb, final)
    if key not in _CACHE:
        _CACHE[key] = build(layers, nb, final)
    return _CACHE[key]


def _in_maps(inputs, xs, ctxs, n, nb):
    f = lambda a: np.ascontiguousarray(np.asarray(a, dtype=np.float32))
    shared = dict(
        c_ctx=f(inputs["c_ctx"]).reshape(1, D),
        w_mod=f(inputs["w_mod"]), b_mod=f(inputs["b_mod"]), norm_g=f(inputs["norm_g"]), w_o=f(inputs["w_o"]),
        a_w_in=f(inputs["a_w_in"])[0], b_w_in=f(inputs["b_w_in"])[0],
        a_lam=np.concatenate([f(inputs["a_lambda_q1"]), f(inputs["a_lambda_k1"]), f(inputs["a_lambda_q2"]), f(inputs["a_lambda_k2"])], axis=0),
        a_subln_g=f(inputs["a_subln_g"]).reshape(1, 128), b_sink=f(inputs["b_sink"]).reshape(1, 16),
        final_g=f(inputs["final_g"]).reshape(1, D),
    )
    shared.update(_consts())
    c = f(inputs["c"])
    maps = []
    for i in range(n):
        m = dict(shared)
        m["x"] = np.ascontiguousarray(xs[i * nb:(i + 1) * nb])
        m["ctx"] = np.ascontiguousarray(ctxs[i * nb:(i + 1) * nb])
        m["c"] = np.ascontiguousarray(c[i * nb:(i + 1) * nb])
        maps.append(m)
    return maps


def kernel(**inputs):
    x = np.asarray(inputs["x"], dtype=np.float32)
    ctx = np.asarray(inputs["ctx"], dtype=np.float32)
    nb = x.shape[0] // N_CORES
    nc = _program((0, 1), nb, True)
    res = run_bass_kernel_spmd(nc, _in_maps(inputs, x, ctx, N_CORES, nb), core_ids=list(range(N_CORES)))
    return np.concatenate([r["y"] for r in res.results], axis=0).astype(np.float32)
```

```python
import contextlib
import math

import numpy as np

import concourse.bass as bass
import concourse.mybir as mybir
from concourse.bass_utils import run_bass_kernel_spmd

F32 = mybir.dt.float32
BF = mybir.dt.bfloat16
AF = mybir.ActivationFunctionType
ALU = mybir.AluOpType
AX = mybir.AxisListType

N_CORES = 8
D = 1024
S = 2048
NCTX = 256
NT = 18
GRID_W = 64
NORM_EPS = 1e-6
SUBLN_EPS = 1e-5
NEG = -30000.0


class _Op:
    __slots__ = ("idx", "eng", "fn", "deps", "signal", "dma", "sem", "val")

    def __init__(self, idx, eng, fn, dma):
        self.idx = idx
        self.eng = eng
        self.fn = fn
        self.deps = set()
        self.signal = False
        self.dma = dma
        self.sem = None
        self.val = 0


class _Rec:
    def __init__(self):
        self.call = None

    def __getattr__(self, name):
        def f(*a, **k):
            assert self.call is None
            self.call = (name, a, k)
            return self
        return f


def _replay(call):
    return lambda eng: getattr(eng, call[0])(*call[1], **call[2])


class Prog:
    ENGINES = ("pe", "act", "dve", "pool", "sp")
    DMA_POOL = {"sp": 12, "pool": 8}

    def __init__(self, nc):
        self.nc = nc
        self.ops = []
        self.by_eng = {e: [] for e in self.ENGINES}
        self.last_writer = {}
        self.readers = {}
        self.open_dmas = []

    def add(self, eng, fn, reads=(), writes=(), dma=False):
        rec = _Rec()
        fn(rec)
        op = _Op(len(self.ops), eng, _replay(rec.call), dma)
        ps_r = [k for k in reads if isinstance(k, tuple) and k[0] == "ps"]
        if ps_r:
            reads = [k for k in reads if k not in ps_r]
            writes = list(writes) + ps_r
        deps = set()
        for k in reads:
            w = self.last_writer.get(k)
            if w is not None:
                deps.add(w)
        for k in writes:
            w = self.last_writer.get(k)
            if w is not None:
                deps.add(w)
            for r in self.readers.get(k, ()):
                deps.add(r)
        for k in reads:
            self.readers.setdefault(k, []).append(op.idx)
        for k in writes:
            self.last_writer[k] = op.idx
            self.readers[k] = []
        for d in deps:
            p = self.ops[d]
            if eng == "pe" and p.eng == "pe" and not p.dma and not dma:
                continue
            op.deps.add(d)
            p.signal = True
        self.ops.append(op)
        self.by_eng[eng].append(op)
        if dma:
            self.open_dmas.append(op.idx)
        return op.idx

    def wait_for(self, eng, idxs):
        op = _Op(len(self.ops), eng, None, False)
        for d in idxs:
            op.deps.add(d)
            self.ops[d].signal = True
        self.ops.append(op)
        self.by_eng[eng].append(op)

    def barrier(self):
        last = []
        for e in self.ENGINES:
            for op in reversed(self.by_eng[e]):
                if op.fn is not None and not op.dma:
                    last.append(op.idx)
                    break
        last += self.open_dmas
        self.open_dmas = []
        for e in self.ENGINES:
            self.wait_for(e, last)

    def emit(self):
        nc = self.nc
        with contextlib.ExitStack() as st:
            sems = {e: st.enter_context(nc.semaphore("s_" + e)) for e in ("pe", "act", "dve", "pool")}
            dsems = {e: [st.enter_context(nc.semaphore("d_%s%d" % (e, i))) for i in range(n)]
                     for e, n in self.DMA_POOL.items()}
            cnt = {e: 0 for e in self.ENGINES}
            dcnt = {e: 0 for e in self.ENGINES}
            hist = {e: [] for e in self.ENGINES}
            for e in self.ENGINES:
                for op in self.by_eng[e]:
                    if op.fn is None:
                        continue
                    if op.dma:
                        pool = dsems[e]
                        j = dcnt[e]
                        dcnt[e] += 1
                        op.sem = pool[j % len(pool)]
                        op.val = 16 * (j // len(pool) + 1)
                        op.signal = True
                        if j >= len(pool):
                            op.deps.add(hist[e][j - len(pool)])
                        hist[e].append(op.idx)
                    elif op.signal:
                        cnt[e] += 1
                        op.sem = sems[e]
                        op.val = cnt[e]
            block = st.enter_context(nc.Block())

            def run(e):
                def body(eng):
                    waited = {}
                    for op in self.by_eng[e]:
                        need = {}
                        for d in op.deps:
                            p = self.ops[d]
                            if need.get(p.sem.num, (None, 0))[1] < p.val:
                                need[p.sem.num] = (p.sem, p.val)
                        for key, (s, v) in need.items():
                            if waited.get(key, 0) < v:
                                eng.wait_ge(s, v)
                                waited[key] = v
                        if op.fn is None:
                            continue
                        ins = op.fn(eng)
                        if op.signal:
                            ins.then_inc(op.sem, 16 if op.dma else 1)
                return body

            block.tensor(run("pe"))
            block.scalar(run("act"))
            block.vector(run("dve"))
            block.gpsimd(run("pool"))
            block.sync(run("sp"))


def _ap(base, off, dims):
    return bass.AP(base.tensor, base.offset + off, [list(base.ap[0])] + [list(d) for d in dims])


class _Stop(Exception):
    pass


STATS = {}
QK256 = True
LOOKAHEAD = 2


def build(layers=(0, 1), nb=4, final=True, stop=None):
    nc = bass.Bass("TRN2", target_bir_lowering=False)

    def din(name, shape):
        return nc.dram_tensor(name, shape, F32, kind="ExternalInput").ap()

    x_d = din("x", [nb, S, D])
    ctx_d = din("ctx", [nb, NCTX, D])
    c_d = din("c", [nb, D])
    cctx_d = din("c_ctx", [1, D])
    wmod_d = din("w_mod", [2, D, 3 * D])
    bmod_d = din("b_mod", [2, 3 * D])
    ng_d = din("norm_g", [2, D])
    wo_d = din("w_o", [2, D, D])
    win_d = [din("a_w_in", [D, 4096]), din("b_w_in", [D, 2560])]
    lam_d = din("a_lam", [4, 64])
    sg_d = din("a_subln_g", [1, 128])
    sink_d = din("b_sink", [1, 16])
    fg_d = din("final_g", [1, D])
    ident_d = din("k_ident", [128, 128])
    cos_d = din("k_cos", [128, 16, 64])
    sin_d = din("k_sin", [128, 16, 64])
    mask_d = din("k_mask", [128, 2, 128])
    y_d = nc.dram_tensor("y", [nb, S, D], F32, kind="ExternalOutput").ap()
    yc_d = None
    if not final:
        yc_d = nc.dram_tensor("yc", [nb, NCTX, D], F32, kind="ExternalOutput").ap()
    scr_d = nc.dram_tensor("scr_gate", [2, 5, D], F32, kind="Internal").ap()
    dbg_d = nc.dram_tensor("dbg", [128, 24576], F32, kind="ExternalOutput").ap() if stop else None

    P = Prog(nc)
    add = P.add

    def bcast_rows(src2d, row, nparts, n):
        a = src2d[row:row + 1, 0:n]
        return bass.AP(a.tensor, a.offset, [[0, nparts], [1, n]])

    with contextlib.ExitStack() as st:
        def sb(name, shape, dt):
            return st.enter_context(nc.sbuf_tensor(name, shape, dt))

        idf = sb("idf", [128, 128], F32)
        idb = sb("idb", [128, 128], BF)
        cosT = sb("cosT", [128, 16, 64], F32)
        sinT = sb("sinT", [128, 16, 64], F32)
        maskb = sb("maskb", [128, 2, 512], BF)
        ABcol = sb("ABcol", [128, 2, 2, 8, 5], F32)
        mhalf = sb("mhalf", [128, 1], F32)
        neglam = sb("neglam", [128, 1], F32)
        esl = sb("esl", [128, 16], F32)
        mB = sb("mB", [128, 128], F32)
        gB2 = sb("gB2", [128, D], F32)
        psA = st.enter_context(nc.psum_tensor("psA", [128, 4, 512], F32))
        psB = st.enter_context(nc.psum_tensor("psB", [128, 4, 512], F32))

        def bank(i):
            return (psA if i < 4 else psB)[:, i % 4, :]

        R = sb("R", [128, NT, D], F32)
        hT = sb("hT", [128, 8, NT * 128], BF)
        KT = sb("KT", [128, 2, NT * 128], BF)
        V = sb("V", [128, NT, 2, 130], BF)
        QT = [sb("QT%d" % i, [128, 2, 512], BF) for i in range(2)]
        GS = [sb("GS%d" % i, [128, 4, 256], BF) for i in range(3)]
        OG = GS
        ET = [sb("ET%d" % i, [128, 2, 512], BF) for i in range(3)]
        Wkv = sb("Wkv", [128, 8, 512], BF)
        Wqg = sb("Wqg", [128, 8, 512], BF)
        Wo = sb("Wo", [128, 2, D], BF)
        gateB = sb("gateB", [128, D], F32)
        big = [sb("big%d" % i, [128, D], F32) for i in range(2)]
        stat = sb("stat", [128, 4, NT], F32)
        ta = [sb("ta%d" % i, [128, 256], F32) for i in range(2)]
        tb = [sb("tb%d" % i, [128, 256], F32) for i in range(2)]
        rot = [sb("rot%d" % i, [128, 256], BF) for i in range(4)]
        tg = [sb("tg%d" % i, [128, 256], F32) for i in range(2)]
        ug = tg
        e0 = sb("e0", [128, 512], F32)
        e1 = sb("e1", [128, 512], F32)
        e2 = sb("e2", [128, 512], F32)
        est = sb("est", [128, 32], F32)
        OGT = [sb("OGT%d" % i, [128, 2, 128], BF) for i in range(2)]
        junk = sb("junk", [128, D], BF)
        sT = sb("sT", [128, 8, 5], F32)
        lq = tb[0][:].rearrange("p (a d) -> p a d", d=64)
        lt = sb("lt", [128, 8], F32)
        sgB = tb[1][:, 0:128]
        es16 = tb[1][:, 128:144]
        maskf = ta[0][:].rearrange("p (a d) -> p a d", d=128)

        def f32view(t):
            return t[:].rearrange("p a b -> p (a b)").bitcast(F32)

        crow = big[0][0:5, :]
        erow = big[1][0:5, :]
        wch = [R[:, 4 * i:4 * i + 4, :].rearrange("p t (a n) -> p (t a) n", n=512) for i in range(2)]
        rch = [e0[0:5, :], e1[0:5, :]]
        bch = [f32view(GS[i])[0:5, :] for i in range(2)]
        gch = [f32view(ET[i])[0:5, :] for i in range(2)]
        ach = [e2[0:5, :], f32view(QT[0])[0:5, :]]

        add("pool", lambda e: e.memset(big[0][0:32, :], 0.0), writes=["crow"])
        add("pool", lambda e: e.memset(mhalf[:], -0.5), writes=["mhalf"])
        add("sp", lambda e: e.dma_start(out=idf[:], in_=ident_d), writes=["idf"], dma=True)
        add("sp", lambda e: e.dma_start(out=big[0][0:nb, :], in_=c_d), reads=["crow"], writes=["crow"], dma=True)
        add("sp", lambda e: e.dma_start(out=big[0][4:5, :], in_=cctx_d), reads=["crow"], writes=["crow4"], dma=True)
        add("sp", lambda e: e.dma_start(out=cosT[:], in_=cos_d), writes=["cos"], dma=True)
        add("sp", lambda e: e.dma_start(out=sinT[:], in_=sin_d), writes=["sin"], dma=True)
        add("sp", lambda e: e.dma_start(out=maskf, in_=mask_d), writes=["maskf"], dma=True)
        for i in range(4):
            add("sp", lambda e, i=i: e.dma_start(out=lq[:, i, :], in_=bcast_rows(lam_d, i, 128, 64)), writes=[("lq", i)], dma=True)
        add("sp", lambda e: e.dma_start(out=sgB, in_=bcast_rows(sg_d, 0, 128, 128)), writes=["sgB"], dma=True)
        add("sp", lambda e: e.dma_start(out=es16, in_=bcast_rows(sink_d, 0, 128, 16)), writes=["es16"], dma=True)
        add("dve", lambda e: e.tensor_copy(out=idb[:], in_=idf[:]), reads=["idf"], writes=["idb"])
        add("dve", lambda e: e.tensor_copy(out=maskb[:].rearrange("p t (s q) -> p t s q", s=4),
                                           in_=_ap(maskf[:, 0, :], 0, [[128, 2], [0, 4], [1, 128]])), reads=["maskf"], writes=["maskb"])
        add("act", lambda e: e.activation(out=erow, in_=crow, func=AF.Exp, scale=-1.0), reads=["crow", "crow4"], writes=["erow"])
        add("dve", lambda e: e.tensor_scalar(out=erow, in0=erow, scalar1=1.0, scalar2=None, op0=ALU.add), reads=["erow"], writes=["erow"])
        add("dve", lambda e: e.reciprocal(out=erow, in_=erow), reads=["erow"], writes=["erow"])
        add("dve", lambda e: e.tensor_tensor(out=erow, in0=erow, in1=crow, op=ALU.mult), reads=["erow", "crow", "crow4"], writes=["erow"])
        for kc in range(8):
            add("pe", lambda e, kc=kc: e.transpose(out=bank(0)[:, kc * 5:kc * 5 + 5], in_=big[1][0:5, kc * 128:(kc + 1) * 128], identity=idf[0:5, 0:5]),
                reads=["erow", "idf"], writes=[("ps", 0)])
        add("dve", lambda e: e.tensor_copy(out=sT[:].rearrange("p a b -> p (a b)"), in_=bank(0)[:, 0:40]), reads=[("ps", 0)], writes=["sT"])
        ci = 0
        for l in range(2):
            for cc in range(6):
                wb_ = ci % 2
                bk = 1 + ci % 2
                kind = cc // 2
                add("sp", lambda e, l=l, cc=cc, wb_=wb_: e.dma_start(
                    out=wch[wb_], in_=wmod_d[l, :, cc * 512:(cc + 1) * 512].rearrange("(kc p) n -> p kc n", p=128)),
                    writes=[("wch", wb_)], dma=True)
                add("sp", lambda e, l=l, cc=cc, wb_=wb_: e.dma_start(out=bch[wb_], in_=_ap(bmod_d[l:l + 1, cc * 512:(cc + 1) * 512], 0, [[1, 512]])
                                                                      if False else bass.AP(bmod_d.tensor, l * 3 * D + cc * 512, [[0, 5], [1, 512]])),
                    writes=[("bch", wb_)], dma=True)
                if kind == 1:
                    add("sp", lambda e, l=l, cc=cc, wb_=wb_: e.dma_start(out=gch[wb_], in_=bass.AP(ng_d.tensor, l * D + (cc - 2) * 512, [[0, 5], [1, 512]])),
                        writes=[("gch", wb_)], dma=True)
                for kc in range(8):
                    add("pe", lambda e, kc=kc, wb_=wb_, bk=bk: e.matmul(bank(bk)[0:5, :], lhsT=sT[:, kc, :], rhs=wch[wb_][:, kc, :],
                                                                         start=(kc == 0), stop=(kc == 7)),
                        reads=["sT", ("wch", wb_)], writes=[("ps", bk)])
                add("dve", lambda e, wb_=wb_, bk=bk: e.tensor_tensor(out=rch[wb_], in0=bank(bk)[0:5, :], in1=bch[wb_], op=ALU.add),
                    reads=[("ps", bk), ("bch", wb_)], writes=[("rch", wb_)])
                if kind == 2:
                    add("sp", lambda e, l=l, cc=cc, wb_=wb_: e.dma_start(out=scr_d[l, :, (cc - 4) * 512:(cc - 3) * 512], in_=rch[wb_]),
                        reads=[("rch", wb_)], writes=[("scr", l)], dma=True)
                    ci += 1
                    continue
                src = rch[wb_]
                skey = ("rch", wb_)
                if kind == 1:
                    add("dve", lambda e, wb_=wb_: e.scalar_tensor_tensor(out=ach[wb_], in0=rch[wb_], scalar=1.0, in1=gch[wb_], op0=ALU.add, op1=ALU.mult),
                        reads=[("rch", wb_), ("gch", wb_)], writes=[("ach", wb_)])
                    src = ach[wb_]
                    skey = ("ach", wb_)
                ab = 0 if kind == 1 else 1
                for k4 in range(4):
                    kc = (cc % 2) * 4 + k4
                    col = ((l * 2 + ab) * 8 + kc) * 5
                    add("pe", lambda e, src=src, col=col, k4=k4: e.transpose(out=bank(3)[:, col:col + 5], in_=src[:, k4 * 128:(k4 + 1) * 128],
                                                                            identity=idf[0:5, 0:5]),
                        reads=[skey, "idf"], writes=[("ps", 3)])
                ci += 1
        add("dve", lambda e: e.tensor_copy(out=ABcol[:].rearrange("p a b c d -> p (a b c d)"), in_=bank(3)[:, 0:160]), reads=[("ps", 3)], writes=["ABcol"])
        lam_init = 0.8 - 0.6 * math.exp(-0.3 * 0)
        add("dve", lambda e: e.tensor_tensor(out=lq[:, 0, :], in0=lq[:, 0, :], in1=lq[:, 1, :], op=ALU.mult), reads=[("lq", 0), ("lq", 1)], writes=[("lq", 0)])
        add("dve", lambda e: e.tensor_tensor(out=lq[:, 2, :], in0=lq[:, 2, :], in1=lq[:, 3, :], op=ALU.mult), reads=[("lq", 2), ("lq", 3)], writes=[("lq", 2)])
        add("dve", lambda e: e.tensor_reduce(out=lt[:, 0:1], in_=lq[:, 0, :], axis=AX.X, op=ALU.add), reads=[("lq", 0)], writes=["lt0"])
        add("dve", lambda e: e.tensor_reduce(out=lt[:, 1:2], in_=lq[:, 2, :], axis=AX.X, op=ALU.add), reads=[("lq", 2)], writes=["lt1"])
        add("act", lambda e: e.activation(out=lt[:, 2:4], in_=lt[:, 0:2], func=AF.Exp), reads=["lt0", "lt1"], writes=["lt2"])
        add("dve", lambda e: e.tensor_tensor(out=lt[:, 4:5], in0=lt[:, 3:4], in1=lt[:, 2:3], op=ALU.subtract), reads=["lt2"], writes=["lt4"])
        add("dve", lambda e: e.tensor_scalar(out=neglam[:], in0=lt[:, 4:5], scalar1=-lam_init, scalar2=None, op0=ALU.add), reads=["lt4"], writes=["neglam"])
        add("act", lambda e: e.activation(out=es16, in_=es16, func=AF.Exp), reads=["es16"], writes=["es16"])
        add("dve", lambda e: e.tensor_copy(out=esl[:].rearrange("p (g h b) -> p g h b", g=4, h=2), in_=_ap(es16, 0, [[4, 4], [1, 2], [2, 2]])),
            reads=["es16"], writes=["esl"])
        sgc = 0.5 * (1.0 - lam_init)
        add("dve", lambda e: e.tensor_scalar(out=mB[:], in0=sgB, scalar1=sgc, scalar2=None, op0=ALU.mult), reads=["sgB"], writes=["mB"])
        P.barrier()

        ctr = {}

        def nxt(name, n):
            v = ctr.get(name, 0)
            ctr[name] = v + 1
            return v % n

        gp_set = [[0, 1, 2, 3]]

        def gp():
            st_ = gp_set[0]
            return st_[nxt("gp", len(st_))]

        def rstd(tag, col, n_inv, eps, src_keys):
            add("dve", lambda e: e.tensor_scalar(out=stat[:, 2, col:col + 1], in0=stat[:, 0, col:col + 1], scalar1=n_inv, scalar2=eps,
                                                 op0=ALU.mult, op1=ALU.add), reads=src_keys, writes=[("st2", col)])
            add("pool", lambda e: e.tensor_tensor(out=stat[:, 1, col:col + 1], in0=stat[:, 2, col:col + 1], in1=mhalf[:], op=ALU.pow),
                reads=[("st2", col), "mhalf"], writes=[("st1", col)])

        def rope(src, width, ltile, out_ap, out_keys, src_keys):
            hm = width // 64
            i = nxt("rope", 2)
            add("dve", lambda e: e.tensor_tensor(out=ta[i][:, 0:width].rearrange("p (h d) -> p h d", d=64), in0=src.rearrange("p (h d) -> p h d", d=64),
                                                 in1=_ap(cosT[:, ltile, :], 0, [[0, hm], [1, 64]]), op=ALU.mult),
                reads=src_keys + ["cos"], writes=[("ta", i)])
            for rc in range(2):
                add("dve", lambda e, rc=rc: e.tensor_tensor(
                    out=_ap(tb[i][:, 0:width], rc * 32, [[64, hm], [16, 2], [1, 16]]),
                    in0=_ap(src, rc * 32 + 16, [[64, hm], [-16, 2], [1, 16]]),
                    in1=_ap(sinT[:, ltile, :], rc * 32, [[0, hm], [16, 2], [1, 16]]), op=ALU.mult),
                    reads=src_keys + ["sin"], writes=[("tb", i, rc)])
            add("pool", lambda e: e.tensor_tensor(out=out_ap, in0=ta[i][:, 0:width], in1=tb[i][:, 0:width], op=ALU.add),
                reads=[("ta", i), ("tb", i, 0), ("tb", i, 1)], writes=out_keys)

        def wcols(l, g):
            if l == 0:
                return g * 256, 1024 + g * 256, 2048 + g * 256, 3072 + g * 256, 256
            return g * 256, 1024 + g * 64, 1280 + g * 64, 1536 + g * 256, 64

        def wld(l, dst, c0, n):
            w = win_d[l]
            return lambda e: e.dma_start(out=dst, in_=w[:, c0:c0 + n].rearrange("(kc p) n -> p kc n", p=128))

        def load_kv(l, g):
            qc, kc_, vc, gc, kw = wcols(l, g)
            add("pool", wld(l, Wkv[:, :, 0:kw], kc_, kw), writes=[("Wkv", 0)], dma=True)
            add("pool", wld(l, Wkv[:, :, kw:2 * kw], vc, kw), writes=[("Wkv", 1)], dma=True)

        def load_qg(l, g):
            qc, kc_, vc, gc, kw = wcols(l, g)
            add("pool", wld(l, Wqg[:, :, 0:256], qc, 256), writes=[("Wqg", 0)], dma=True)
            add("pool", wld(l, Wqg[:, :, 256:512], gc, 256), writes=[("Wqg", 1)], dma=True)

        def load_wo(l, g):
            add("pool", lambda e: e.dma_start(out=Wo[:], in_=wo_d[l, g * 256:(g + 1) * 256, :].rearrange("(kc p) n -> p kc n", p=128)),
                writes=["Wo"], dma=True)

        def phase0(l, b):
            xbs = {}

            def s1(tt):
                add("act", lambda e: e.activation(out=junk[:], in_=R[:, tt, :], func=AF.Square, accum_out=stat[:, 0, tt:tt + 1]),
                    reads=[("R", tt)], writes=[("st0", tt), "junk"])
                rstd("n", tt, 1.0 / D, NORM_EPS, [("st0", tt)])

            def s3(tt):
                xb = nxt("big", 2)
                xbs[tt] = xb
                add("pool", lambda e: e.tensor_tensor(out=big[xb][:], in0=R[:, tt, :], in1=_ap(stat[:, 1, tt:tt + 1], 0, [[0, D]]), op=ALU.mult),
                    reads=[("R", tt), ("st1", tt)], writes=[("big", xb)])

            def s4(tt):
                bb = 4 if tt < 2 else b
                xb = xbs[tt]
                for half in range(2):
                    bk = gp()
                    for k4 in range(4):
                        kc = half * 4 + k4
                        add("pe", lambda e: e.transpose(out=bank(bk)[:, k4 * 128:(k4 + 1) * 128],
                                                        in_=big[xb][:, kc * 128:(kc + 1) * 128], identity=idf[:]),
                            reads=[("big", xb), "idf"], writes=[("ps", bk)])
                    for k4 in range(4):
                        kc = half * 4 + k4
                        dst = hT[:, kc, tt * 128:(tt + 1) * 128]
                        src = bank(bk)[:, k4 * 128:(k4 + 1) * 128]
                        a_ = ABcol[:, l, 0, kc, bb:bb + 1]
                        b_ = ABcol[:, l, 1, kc, bb:bb + 1]
                        if k4 == 0:
                            add("act", lambda e: e.activation(out=dst, in_=src, func=AF.Identity, scale=a_, bias=b_),
                                reads=[("ps", bk), "ABcol"], writes=[("hT", tt, kc)])
                        else:
                            add("dve", lambda e: e.tensor_scalar(out=dst, in0=src, scalar1=a_, scalar2=b_, op0=ALU.mult, op1=ALU.add),
                                reads=[("ps", bk), "ABcol"], writes=[("hT", tt, kc)])

            for i in range(NT + 2):
                if i < NT:
                    s1(i)
                if 0 <= i - 1 < NT:
                    s3(i - 1)
                if 0 <= i - 2 < NT:
                    s4(i - 2)

        def hT_keys(tt):
            return [("hT", tt, kc) for kc in range(8)]

        def phase1(l, g):
            kw = 256 if l == 0 else 64
            nblk = 2 if l == 0 else 1
            vw = 128 if l == 0 else 64
            nv = 2 if l == 0 else 1
            ris = {}

            def mm(tt):
                bk = gp()
                for kc in range(8):
                    add("pe", lambda e: e.matmul(bank(bk)[:, 0:2 * kw], lhsT=hT[:, kc, tt * 128:(tt + 1) * 128],
                                                 rhs=Wkv[:, kc, 0:2 * kw], start=(kc == 0), stop=(kc == 7)),
                        reads=[("hT", tt, kc), ("Wkv", 0), ("Wkv", 1)], writes=[("ps", bk)])
                add("act", lambda e: e.activation(out=V[:, tt, 0:nv, 0:vw], in_=bank(bk)[:, kw:2 * kw].rearrange("p (h d) -> p h d", h=nv),
                                                  func=AF.Copy),
                    reads=[("ps", bk)], writes=[("V", tt)])
                ri = nxt("rot", 4)
                ris[tt] = ri
                if tt < 2:
                    add("dve", lambda e: e.tensor_copy(out=rot[ri][:, 0:kw], in_=bank(bk)[:, 0:kw]), reads=[("ps", bk)], writes=[("rot", ri)])
                else:
                    rope(bank(bk)[:, 0:kw], kw, tt - 2, rot[ri][:, 0:kw], [("rot", ri)], [("ps", bk)])
                if l == 1:
                    add("pool", lambda e: e.tensor_copy(out=rot[ri][:, 64:128], in_=rot[ri][:, 0:64]), reads=[("rot", ri)], writes=[("rotd", ri)])

            def tr(tt):
                ri = ris[tt]
                bk2 = gp()
                pv = bank(bk2).bitcast(BF)
                for blk in range(nblk):
                    add("pe", lambda e: e.transpose(out=pv[:, blk * 128:(blk + 1) * 128], in_=rot[ri][:, blk * 128:(blk + 1) * 128], identity=idb[:]),
                        reads=[("rot", ri), ("rotd", ri), "idb"], writes=[("ps", bk2)])
                add("act", lambda e: e.activation(out=KT[:, 0:nblk, tt * 128:(tt + 1) * 128],
                                                  in_=pv[:, 0:nblk * 128].rearrange("p (a b) -> p a b", a=nblk), func=AF.Copy),
                    reads=[("ps", bk2)], writes=[("KT", tt)])

            LAG = 3
            for i in range(NT + LAG):
                if i < NT:
                    mm(i)
                if 0 <= i - LAG < NT:
                    tr(i - LAG)

        def phase2a_mm(l, g, cb, q_tiles):
            ris = []
            for j, tt in enumerate(q_tiles):
                bk = gp()
                for kc in range(8):
                    add("pe", lambda e, tt=tt, kc=kc, bk=bk: e.matmul(bank(bk)[:, :], lhsT=hT[:, kc, tt * 128:(tt + 1) * 128],
                                                                       rhs=Wqg[:, kc, :], start=(kc == 0), stop=(kc == 7)),
                        reads=[("hT", tt, kc), ("Wqg", 0), ("Wqg", 1)], writes=[("ps", bk)])
                gi = nxt("tg", 2)
                add("act", lambda e, bk=bk, gi=gi: e.activation(out=tg[gi][:], in_=bank(bk)[:, 256:512], func=AF.Tanh, scale=0.5),
                    reads=[("ps", bk)], writes=[("tg", gi)])
                if l == 0:
                    add("dve", lambda e, bk=bk, gi=gi: e.scalar_tensor_tensor(out=tg[gi][:], in0=tg[gi][:], scalar=1.0, in1=bank(bk)[:, 256:512],
                                                                              op0=ALU.add, op1=ALU.mult),
                        reads=[("ps", bk), ("tg", gi)], writes=[("tg", gi)])
                    add("pool", lambda e, gi=gi, j=j: e.tensor_tensor(out=GS[cb][:, j, :].rearrange("p (h d) -> p h d", h=2),
                                                                      in0=tg[gi][:].rearrange("p (h d) -> p h d", h=2),
                                                                      in1=_ap(mB[:], 0, [[0, 2], [1, 128]]), op=ALU.mult),
                        reads=[("tg", gi), "mB"], writes=[("GS", cb, j, 0), ("GS", cb, j, 1)])
                else:
                    add("dve", lambda e, bk=bk, gi=gi: e.scalar_tensor_tensor(out=tg[gi][:], in0=tg[gi][:], scalar=1.0, in1=bank(bk)[:, 256:512],
                                                                              op0=ALU.add, op1=ALU.mult),
                        reads=[("ps", bk), ("tg", gi)], writes=[("tg", gi)])
                    add("pool", lambda e, gi=gi, j=j: e.tensor_copy(out=GS[cb][:, j, :], in_=tg[gi][:]),
                        reads=[("tg", gi)], writes=[("GS", cb, j, 0), ("GS", cb, j, 1)])
                ri = nxt("rot", 4)
                ris.append(ri)
                if tt < 2:
                    add("dve", lambda e, bk=bk, ri=ri: e.tensor_copy(out=rot[ri][:], in_=bank(bk)[:, 0:256]), reads=[("ps", bk)], writes=[("rot", ri)])
                else:
                    rope(bank(bk)[:, 0:256], 256, tt - 2, rot[ri][:], [("rot", ri)], [("ps", bk)])
            return ris

        def phase2a_tr(l, g, qb, q_tiles, ris):
            for j, tt in enumerate(q_tiles):
                ri = ris[j]
                bk2 = gp()
                pv = bank(bk2).bitcast(BF)
                for blk in range(2):
                    add("pe", lambda e, ri=ri, blk=blk, pv=pv: e.transpose(out=pv[:, blk * 128:(blk + 1) * 128], in_=rot[ri][:, blk * 128:(blk + 1) * 128],
                                                                           identity=idb[:]),
                        reads=[("rot", ri), "idb"], writes=[("ps", bk2)])
                add("act", lambda e, j=j, pv=pv: e.activation(out=QT[qb][:, :, j * 128:(j + 1) * 128],
                                                              in_=pv[:, 0:256].rearrange("p (a b) -> p a b", a=2), func=AF.Copy),
                    reads=[("ps", bk2)], writes=[("QT", qb, j)])

        def attn0(cb, qb, q_tiles, key_tiles, hook=None):
            J = len(q_tiles)
            NQ = J * 128
            qkeys = [("QT", qb, j) for j in range(J)]
            nk = len(key_tiles)

            def qk(hh, ki):
                kt = key_tiles[ki]
                sbuf_i = nxt("sbuf", 2)
                for m in range(2):
                    add("pe", lambda e: e.matmul(
                        psA[:, 2 * sbuf_i + m, 0:NQ], lhsT=KT[m * 64:(m + 1) * 64, hh, kt * 128:(kt + 1) * 128],
                        rhs=QT[qb][m * 64:(m + 1) * 64, hh, 0:NQ], start=True, stop=True),
                        reads=[("KT", kt)] + qkeys, writes=[("ps", 2 * sbuf_i + m)])
                eb = nxt("et", 3)
                add("act", lambda e: e.activation(out=ET[eb][:, :, 0:NQ], in_=psA[:, 2 * sbuf_i:2 * sbuf_i + 2, 0:NQ],
                                                  func=AF.Exp, scale=0.125),
                    reads=[("ps", 2 * sbuf_i), ("ps", 2 * sbuf_i + 1)], writes=[("ET", eb)])
                return (hh, ki, eb)

            def pv(u):
                hh, ki, eb = u
                kt = key_tiles[ki]
                for j in range(J):
                    for m in range(2):
                        add("pe", lambda e: e.matmul(
                            psB[:, j, m * 256:m * 256 + 130], lhsT=ET[eb][:, m, j * 128:(j + 1) * 128], rhs=V[:, kt, hh, :],
                            start=(ki == 0 and m == 0), stop=(ki == nk - 1), skip_group_check=True),
                            reads=[("ET", eb), ("V", kt)], writes=[("ps", 4 + j)])
                if ki == nk - 1:
                    epi(hh)

            def epi(hh):
                okeys = [("ps", 4 + j) for j in range(J)]
                zv = _ap(psB[:, 0, :], 128, [[512, J], [256, 2]])
                rz = est[:, 0:2 * J].rearrange("p (j m) -> p j m", m=2)
                add("dve", lambda e, zv=zv, rz=rz: e.reciprocal(out=rz, in_=zv), reads=okeys, writes=["rz"])
                add("dve", lambda e: e.tensor_scalar(out=_ap(est[:, 0:1], 1, [[2, J]]), in0=_ap(est[:, 0:1], 1, [[2, J]]), scalar1=neglam[:], scalar2=None,
                                                     op0=ALU.mult), reads=["rz", "neglam"], writes=["rz"])
                for m, dst in ((0, e0), (1, e1)):
                    add("dve", lambda e, m=m, dst=dst: e.tensor_tensor(out=dst[:, 0:NQ].rearrange("p (j d) -> p j d", d=128),
                                                                       in0=_ap(psB[:, 0, :], m * 256, [[512, J], [1, 128]]),
                                                                       in1=_ap(est[:, 0:1], m, [[2, J], [0, 128]]), op=ALU.mult),
                        reads=okeys + ["rz"], writes=[("e", m)])
                add("dve", lambda e: e.tensor_tensor(out=e2[:, 0:NQ], in0=e0[:, 0:NQ], in1=e1[:, 0:NQ], op=ALU.add),
                    reads=[("e", 0), ("e", 1)], writes=[("e", 2)])
                add("dve", lambda e: e.tensor_tensor(out=e0[:, 0:NQ], in0=e2[:, 0:NQ], in1=e2[:, 0:NQ], op=ALU.mult), reads=[("e", 2)], writes=[("e", 0)])
                add("dve", lambda e: e.tensor_reduce(out=est[:, 8:8 + J], in_=e0[:, 0:NQ].rearrange("p (j d) -> p j d", d=128), axis=AX.X, op=ALU.add),
                    reads=[("e", 0)], writes=["ssq"])
                add("dve", lambda e: e.tensor_scalar(out=est[:, 12:12 + J], in0=est[:, 8:8 + J], scalar1=1.0 / 128, scalar2=SUBLN_EPS,
                                                     op0=ALU.mult, op1=ALU.add), reads=["ssq"], writes=["ssq2"])
                add("pool", lambda e: e.tensor_tensor(out=est[:, 16:16 + J], in0=est[:, 12:12 + J], in1=_ap(mhalf[:], 0, [[0, J]]), op=ALU.pow),
                    reads=["ssq2", "mhalf"], writes=["srs"])
                add("dve", lambda e: e.tensor_tensor(out=e1[:, 0:NQ].rearrange("p (j d) -> p j d", d=128),
                                                     in0=e2[:, 0:NQ].rearrange("p (j d) -> p j d", d=128),
                                                     in1=_ap(est[:, 16:17], 0, [[1, J], [0, 128]]), op=ALU.mult),
                    reads=[("e", 2), "srs"], writes=[("e", 1)])
                add("dve", lambda e: e.tensor_tensor(out=OG[cb][:, 0:J, hh * 128:(hh + 1) * 128],
                                                             in0=e1[:, 0:NQ].rearrange("p (j d) -> p j d", d=128),
                                                             in1=GS[cb][:, 0:J, hh * 128:(hh + 1) * 128], op=ALU.mult),
                    reads=[("e", 1)] + [("GS", cb, j, hh) for j in range(J)], writes=[("GS", cb, j, hh) for j in range(J)])
                check("ep%d" % hh, [(est[:], ["rz", "ssq", "ssq2", "srs"]), (e2[:], [("e", 2)]), (e1[:], [("e", 1)]),
                                    (e0[:], [("e", 0)]),
                                    (GS[cb][:].rearrange("p a b -> p (a b)"), [("GS", cb, j, h) for j in range(J) for h in range(2)])])

            pend = []
            for hh in range(2):
                for ki in range(nk):
                    pend.append(qk(hh, ki))
                    if len(pend) > LOOKAHEAD:
                        pv(pend.pop(0))
                    if hook is not None and hh == 0 and ki == min(12, nk - 1):
                        hook()
            while pend:
                pv(pend.pop(0))

        def attn1(g, cb, qb, q_tiles, hook=None):
            units = []
            for j, tt in enumerate(q_tiles):
                i = tt - 2
                keys = [(0, None), (1, None)]
                if i > 0:
                    keys.append((tt - 1, 0))
                keys.append((tt, None))
                if i < 15:
                    keys.append((tt + 1, 1))
                ob = nxt("ob", 2)
                for ki, (kt, mk) in enumerate(keys):
                    units.append((j, ob, ki, kt, mk, len(keys)))

            def qk(u):
                j, ob, ki, kt, mk, nk = u
                sb_ = nxt("sbuf", 2)
                for half in range(2):
                    bk = 2 * sb_ + half
                    if mk is not None:
                        add("pe", lambda e: e.matmul(psA[:, bk, 0:256], lhsT=idb[:], rhs=maskb[:, mk, 0:256], start=True, stop=False,
                                                     skip_group_check=True),
                            reads=["idb", "maskb"], writes=[("ps", bk)])
                    if QK256:
                        add("pe", lambda e: e.matmul(
                            psA[:, bk, 0:256].rearrange("p (a b) -> p a b", a=2),
                            lhsT=KT[half * 64:(half + 1) * 64, 0, kt * 128:(kt + 1) * 128],
                            rhs=QT[qb][half * 64:(half + 1) * 64, :, j * 128:(j + 1) * 128],
                            start=(mk is None), stop=True, skip_group_check=True),
                            reads=[("KT", kt), ("QT", qb, j)], writes=[("ps", bk)])
                        continue
                    for blk in range(2):
                        add("pe", lambda e: e.matmul(
                            psA[:, bk, blk * 128:(blk + 1) * 128],
                            lhsT=KT[half * 64:(half + 1) * 64, 0, kt * 128:(kt + 1) * 128],
                            rhs=QT[qb][half * 64:(half + 1) * 64, blk, j * 128:(j + 1) * 128],
                            start=(blk == 0 and mk is None), stop=(blk == 1), skip_group_check=True),
                            reads=[("KT", kt), ("QT", qb, j)], writes=[("ps", bk)])
                eb = nxt("et", 3)
                add("act", lambda e: e.activation(out=ET[eb][:, :, 0:256], in_=psA[:, 2 * sb_:2 * sb_ + 2, 0:256], func=AF.Exp, scale=0.125),
                    reads=[("ps", 2 * sb_), ("ps", 2 * sb_ + 1)], writes=[("ET", eb)])
                return eb

            def pv(u, eb):
                j, ob, ki, kt, mk, nk = u
                for s_ in range(4):
                    add("pe", lambda e: e.matmul(
                        psB[:, ob, s_ * 66:s_ * 66 + 66], lhsT=ET[eb][:, s_ // 2, (s_ % 2) * 128:(s_ % 2 + 1) * 128], rhs=V[:, kt, 0, 0:66],
                        start=(ki == 0 and s_ == 0), stop=(ki == nk - 1), skip_group_check=True),
                        reads=[("ET", eb), ("V", kt)], writes=[("ps", 4 + ob)])
                if ki == nk - 1:
                    epi(j, ob)

            def epi(j, ob):
                okey = [("ps", 4 + ob)]
                zt = est[:, 20:24]
                add("dve", lambda e: e.tensor_tensor(out=zt, in0=_ap(psB[:, ob, :], 64, [[66, 4]]), in1=esl[:, g * 4:(g + 1) * 4], op=ALU.add),
                    reads=okey + ["esl"], writes=["zt"])
                add("dve", lambda e: e.tensor_scalar(out=zt, in0=zt, scalar1=2.0, scalar2=None, op0=ALU.mult), reads=["zt"], writes=["zt"])
                add("dve", lambda e: e.reciprocal(out=est[:, 24:28], in_=zt), reads=["zt"], writes=["rzt"])
                add("dve", lambda e: e.tensor_tensor(out=e0[:, 0:256].rearrange("p (s d) -> p s d", d=64),
                                                     in0=_ap(psB[:, ob, :], 0, [[66, 4], [1, 64]]),
                                                     in1=_ap(est[:, 24:25], 0, [[1, 4], [0, 64]]), op=ALU.mult),
                    reads=okey + ["rzt"], writes=[("e", 0)])
                add("dve", lambda e: e.tensor_tensor(out=OG[cb][:, j, :].rearrange("p (b h d) -> p b h d", b=2, h=2),
                                                     in0=_ap(e0[:, 0:1], 0, [[64, 2], [128, 2], [1, 64]]),
                                                     in1=GS[cb][:, j, :].rearrange("p (b h d) -> p b h d", b=2, h=2), op=ALU.mult),
                    reads=[("e", 0), ("GS", cb, j, 0), ("GS", cb, j, 1)], writes=[("GS", cb, j, 0), ("GS", cb, j, 1)])

            pend = []
            for ui, u in enumerate(units):
                pend.append((u, qk(u)))
                if len(pend) > LOOKAHEAD:
                    pv(*pend.pop(0))
                if hook is not None and ui == min(12, len(units) - 1):
                    hook()
            while pend:
                pv(*pend.pop(0))

        def phase2c(l, g, cb, q_tiles, fin=None):
            ois = {}

            def tr(j):
                bk2 = gp()
                pv = bank(bk2).bitcast(BF)
                oi = nxt("ogt", 2)
                ois[j] = oi
                for blk in range(2):
                    add("pe", lambda e: e.transpose(out=pv[:, blk * 128:(blk + 1) * 128], in_=OG[cb][:, j, blk * 128:(blk + 1) * 128], identity=idb[:]),
                        reads=[("GS", cb, j, 0), ("GS", cb, j, 1), "idb"], writes=[("ps", bk2)])
                add("dve", lambda e: e.tensor_copy(out=OGT[oi][:].rearrange("p a b -> p (a b)"), in_=pv[:, 0:256]),
                    reads=[("ps", bk2)], writes=[("OGT", oi)])

            def op(j):
                tt = q_tiles[j]
                oi = ois[j]
                gtile = gB2 if tt < 2 else gateB
                gkey = "gB2" if tt < 2 else "gateB"
                xb = nxt("big", 2)
                for half in range(2):
                    bk = gp()
                    for k2 in range(2):
                        add("pe", lambda e: e.matmul(bank(bk)[:, :], lhsT=OGT[oi][:, k2, :], rhs=Wo[:, k2, half * 512:(half + 1) * 512],
                                                     start=(k2 == 0), stop=(k2 == 1)),
                            reads=[("OGT", oi), "Wo"], writes=[("ps", bk)])
                    add("dve", lambda e: e.tensor_tensor(out=big[xb][:, half * 512:(half + 1) * 512], in0=bank(bk)[:, :],
                                                         in1=gtile[:, half * 512:(half + 1) * 512], op=ALU.mult),
                        reads=[("ps", bk), gkey], writes=[("big", xb, half)])
                add("pool", lambda e: e.tensor_tensor(out=R[:, tt, :], in0=R[:, tt, :], in1=big[xb][:], op=ALU.add),
                    reads=[("R", tt), ("big", xb, 0), ("big", xb, 1)], writes=[("R", tt)])
                if fin is not None:
                    fin(tt)

            J = len(q_tiles)
            for i in range(J + 1):
                if i < J:
                    tr(i)
                if i >= 1:
                    op(i - 1)

        _orig_add = P.add

        def add(eng, fn, reads=(), writes=(), dma=False):
            def ex(ks):
                out = []
                for k in ks:
                    out.append(k)
                    if isinstance(k, tuple) and k[0] == "big" and len(k) == 2:
                        out += [("big", k[1], 0), ("big", k[1], 1)]
                return out
            return _orig_add(eng, fn, ex(list(reads)), ex(list(writes)), dma)

        out_dmas = []
        dbg_col = [0]

        def dump(ap2d, keys):
            n = ap2d.shape[1]
            c0 = dbg_col[0]
            dbg_col[0] += n
            out_dmas.append(add("pool", lambda e: e.dma_start(out=dbg_d[0:ap2d.shape[0], c0:c0 + n], in_=ap2d, allow_slow_non_contiguous=True), reads=keys, dma=True))
            print("dbg", c0, n, keys[:2])

        def check(tag, items):
            if stop == tag:
                for ap2d, keys in items:
                    dump(ap2d, keys)
                raise _Stop()

        try:
            check("pro", [(ABcol[:].rearrange("p a b c d -> p (a b c d)"), ["ABcol"]), (neglam[:], ["neglam"]), (esl[:], ["esl"]), (mB[:], ["mB"]),
                          (maskb[:].rearrange("p a b -> p (a b)"), ["maskb"]), (idb[:], ["idb"])])
            def load_x(bn, t4):
                add("sp", lambda e: e.dma_start(out=R[:, 2 + 4 * t4:6 + 4 * t4, :],
                                                in_=x_d[bn, t4 * 512:(t4 + 1) * 512, :].rearrange("(t p) d -> p t d", p=128)),
                    writes=[("R", 2 + 4 * t4 + k) for k in range(4)], dma=True)

            def load_ctx(bn):
                add("sp", lambda e: e.dma_start(out=R[:, 0:2, :], in_=ctx_d[bn].rearrange("(t p) d -> p t d", p=128)),
                    writes=[("R", 0), ("R", 1)], dma=True)

            fuse_final = final and (1 in layers)

            def final_tile(b, tt):
                nonlocal prefetched
                xb = nxt("big", 2)
                add("act", lambda e: e.activation(out=junk[:], in_=R[:, tt, :], func=AF.Square, accum_out=stat[:, 0, tt:tt + 1]),
                    reads=[("R", tt)], writes=[("st0", tt), "junk"])
                rstd("f", tt, 1.0 / D, NORM_EPS, [("st0", tt)])
                add("dve", lambda e: e.scalar_tensor_tensor(out=big[xb][:], in0=R[:, tt, :], scalar=stat[:, 1, tt:tt + 1], in1=gB2[:],
                                                            op0=ALU.mult, op1=ALU.mult),
                    reads=[("R", tt), ("st1", tt), "gB2"], writes=[("big", xb)])
                out_dmas.append(add("sp", lambda e: e.dma_start(out=y_d[b, (tt - 2) * 128:(tt - 1) * 128, :], in_=big[xb][:]),
                                    reads=[("big", xb)], dma=True))
                if b + 1 < nb and (tt - 2) % 4 == 3:
                    load_x(b + 1, (tt - 2) // 4)
                    prefetched = True

            prefetched = False
            for b in range(nb):
                if not prefetched:
                    for t4 in range(4):
                        load_x(b, t4)
                    load_ctx(b)
                for l in layers:
                    last = (l == 1)
                    add("sp", lambda e, l=l, b=b: e.dma_start(out=gateB[:], in_=bcast_rows(scr_d[l], b, 128, D)),
                        reads=[("scr", l)], writes=["gateB"], dma=True)
                    if not last:
                        add("sp", lambda e, l=l: e.dma_start(out=gB2[:], in_=bcast_rows(scr_d[l], 4, 128, D)),
                            reads=[("scr", l)], writes=["gB2"], dma=True)
                    ocol = 128 if l == 0 else 64
                    add("pool", lambda e, ocol=ocol: e.memset(V[:, :, :, ocol:ocol + 2], 1.0), writes=[("V", tt) for tt in range(NT)])
                    load_kv(l, 0)
                    load_qg(l, 0)
                    load_wo(l, 0)
                    phase0(l, b)
                    if fuse_final and last:
                        add("sp", lambda e: e.dma_start(out=gB2[:], in_=bcast_rows(fg_d, 0, 128, D)), writes=["gB2"], dma=True)
                        if b + 1 < nb:
                            load_ctx(b + 1)
                    check("p0", [(hT[:].rearrange("p a b -> p (a b)"), [k for tt in range(NT) for k in hT_keys(tt)])])
                    for g in range(4):
                        phase1(l, g)
                        check("p1", [(KT[:].rearrange("p a b -> p (a b)"), [("KT", tt) for tt in range(NT)]),
                                     (V[:].rearrange("p a b c -> p (a b c)"), [("V", tt) for tt in range(NT)])])
                        if g < 3:
                            load_kv(l, g + 1)
                        chunks = []
                        if not last:
                            chunks.append([0, 1])
                        for c4 in range(4):
                            chunks.append([2 + 4 * c4 + k for k in range(4)])
                        gp_set[0] = [6, 7] if l == 1 else [0, 1, 2, 3]
                        fin = None
                        if fuse_final and last and g == 3:
                            fin = (lambda tt, b=b: final_tile(b, tt))
                        cbs = [nxt("cb", 3) for _ in chunks]
                        qbs = [nxt("qb", 2) for _ in chunks]
                        ris0 = phase2a_mm(l, g, cbs[0], chunks[0])
                        phase2a_tr(l, g, qbs[0], chunks[0], ris0)
                        for ci, q_tiles in enumerate(chunks):
                            nxt_ris = None
                            if ci + 1 < len(chunks):
                                nxt_ris = phase2a_mm(l, g, cbs[ci + 1], chunks[ci + 1])
                            hook = None
                            if ci >= 1:
                                hook = (lambda pc=ci - 1: phase2c(l, g, cbs[pc], chunks[pc], fin))
                            if l == 0:
                                attn0(cbs[ci], qbs[ci], q_tiles, [0, 1] if q_tiles[0] < 2 else list(range(NT)), hook)
                            else:
                                attn1(g, cbs[ci], qbs[ci], q_tiles, hook)
                            if nxt_ris is not None:
                                phase2a_tr(l, g, qbs[ci + 1], chunks[ci + 1], nxt_ris)
                        phase2c(l, g, cbs[-1], chunks[-1], fin)
                        gp_set[0] = [0, 1, 2, 3]
                        if g < 3:
                            load_qg(l, g + 1)
                            load_wo(l, g + 1)
                if fuse_final:
                    pass
                else:
                    for t4 in range(4):
                        out_dmas.append(add("sp", lambda e, b=b, t4=t4: e.dma_start(
                            out=y_d[b, t4 * 512:(t4 + 1) * 512, :].rearrange("(t p) d -> p t d", p=128), in_=R[:, 2 + 4 * t4:6 + 4 * t4, :]),
                            reads=[("R", 2 + 4 * t4 + k) for k in range(4)], dma=True))
                    out_dmas.append(add("sp", lambda e, b=b: e.dma_start(out=yc_d[b].rearrange("(t p) d -> p t d", p=128), in_=R[:, 0:2, :]),
                                        reads=[("R", 0), ("R", 1)], dma=True))
        except _Stop:
            pass
        P.wait_for("pool", out_dmas)
        P.wait_for("sp", out_dmas)
        STATS.update({e: len(v) for e, v in P.by_eng.items()})
        STATS["P"] = P
        P.emit()
    return nc


def _consts():
    ident = np.eye(128, dtype=np.float32)
    tok = np.arange(S)
    row = (tok // GRID_W).astype(np.float32)
    col = (tok % GRID_W).astype(np.float32)
    inv = (10000.0 ** (-np.arange(0, 32, 2, dtype=np.float32) / 32.0)).astype(np.float32)
    ar = row[:, None] * inv[None, :]
    ac = col[:, None] * inv[None, :]
    cos64 = np.concatenate([np.cos(ar), np.cos(ar), np.cos(ac), np.cos(ac)], axis=1)
    sin64 = np.concatenate([-np.sin(ar), np.sin(ar), -np.sin(ac), np.sin(ac)], axis=1)
    cosT = np.ascontiguousarray(cos64.reshape(16, 128, 64).transpose(1, 0, 2)).astype(np.float32)
    sinT = np.ascontiguousarray(sin64.reshape(16, 128, 64).transpose(1, 0, 2)).astype(np.float32)
    k = np.arange(128)[:, None]
    q = np.arange(128)[None, :]
    mask = np.zeros((128, 2, 128), np.float32)
    mask[:, 0, :] = np.where(k >= q, 0.0, NEG)
    mask[:, 1, :] = np.where(k <= q, 0.0, NEG)
    return dict(k_ident=ident, k_cos=cosT, k_sin=sinT, k_mask=mask)


_CACHE = {}


def _program(layers, nb, final):
    key = (tuple(layers), nb, final)
    if key not in _CACHE:
        _CACHE[key] = build(layers, nb, final)
    return _CACHE[key]


def _in_maps(inputs, xs, ctxs, n, nb):
    f = lambda a: np.ascontiguousarray(np.asarray(a, dtype=np.float32))
    shared = dict(
        c_ctx=f(inputs["c_ctx"]).reshape(1, D),
        w_mod=f(inputs["w_mod"]), b_mod=f(inputs["b_mod"]), norm_g=f(inputs["norm_g"]), w_o=f(inputs["w_o"]),
        a_w_in=f(inputs["a_w_in"])[0], b_w_in=f(inputs["b_w_in"])[0],
        a_lam=np.concatenate([f(inputs["a_lambda_q1"]), f(inputs["a_lambda_k1"]), f(inputs["a_lambda_q2"]), f(inputs["a_lambda_k2"])], axis=0),
        a_subln_g=f(inputs["a_subln_g"]).reshape(1, 128), b_sink=f(inputs["b_sink"]).reshape(1, 16),
        final_g=f(inputs["final_g"]).reshape(1, D),
    )
    shared.update(_consts())
    c = f(inputs["c"])
    maps = []
    for i in range(n):
        m = dict(shared)
        m["x"] = np.ascontiguousarray(xs[i * nb:(i + 1) * nb])
        m["ctx"] = np.ascontiguousarray(ctxs[i * nb:(i + 1) * nb])
        m["c"] = np.ascontiguousarray(c[i * nb:(i + 1) * nb])
        maps.append(m)
    return maps


def kernel(**inputs):
    x = np.asarray(inputs["x"], dtype=np.float32)
    ctx = np.asarray(inputs["ctx"], dtype=np.float32)
    nb = x.shape[0] // N_CORES
    nc = _program((0, 1), nb, True)
    res = run_bass_kernel_spmd(nc, _in_maps(inputs, x, ctx, N_CORES, nb), core_ids=list(range(N_CORES)))
    return np.concatenate([r["y"] for r in res.results], axis=0).astype(np.float32)
```
